# Optimizing a Trainium2 kernel written in Bass

```python
import math
import jax, jax.numpy as jnp
from jax import lax
import numpy as np

D_MODEL = 2048
BATCH = 4
SEQ = 4096
DEPTH = 4

HEAD_DIM = 64
N_MIX_HEADS = D_MODEL // HEAD_DIM
A_Q_HEADS = N_MIX_HEADS // 2
A_KV_HEADS = A_Q_HEADS // 4
A_WINDOW = 128
B_Q_HEADS = N_MIX_HEADS - A_Q_HEADS
B_KV_HEADS = 2
NSA_CMP_LEN = 32
NSA_CMP_STRIDE = 16
NSA_CMP_HIDDEN = 4 * HEAD_DIM
NSA_SEL_LEN = 64
NSA_TOP_N = 16
NSA_WINDOW = 512
NSA_FORCE_SCORE = 1.0e4
C_HEADS = N_MIX_HEADS
D_FF = 4 * D_MODEL
REL_BUCKETS = 32
REL_MAX_DIST = 1024
REL_HEADS = A_Q_HEADS + B_Q_HEADS
Q_BLOCK = 128
RMS_EPS = 1e-6
ATTN_SCALE = HEAD_DIM ** -0.5
EVEN_SIZES = (A_Q_HEADS * HEAD_DIM, A_KV_HEADS * HEAD_DIM, A_KV_HEADS * HEAD_DIM, B_Q_HEADS * HEAD_DIM) + (B_KV_HEADS * HEAD_DIM,) * 6 + (3 * B_Q_HEADS,)
EVEN_IN = sum(EVEN_SIZES)
EVEN_MIX = (A_Q_HEADS + B_Q_HEADS) * HEAD_DIM
C_MIX = C_HEADS * HEAD_DIM
ODD_IN = 3 * C_MIX + C_HEADS

kernel_name = 'hybrid_swa_nsa_fox_trunk'


def rms_norm(x, g):
    xf = x.astype(jnp.float32)
    y = xf * lax.rsqrt(jnp.mean(xf * xf, axis=-1, keepdims=True) + RMS_EPS)
    return (y * g.astype(jnp.float32)).astype(x.dtype)


def rel_bucket(dist):
    n = jnp.maximum(dist, 0)
    max_exact = REL_BUCKETS // 2
    nf = jnp.maximum(n, 1).astype(jnp.float32)
    large = max_exact + (jnp.log(nf / max_exact) / math.log(REL_MAX_DIST / max_exact) * (REL_BUCKETS - max_exact)).astype(jnp.int32)
    return jnp.where(n < max_exact, n, jnp.minimum(large, REL_BUCKETS - 1))


def head_bias(tab, dist, g, r):
    b = tab[rel_bucket(dist)]
    return jnp.moveaxis(b, -1, 0).reshape(g, r, *dist.shape)


def masked_softmax(s, mask):
    s = jnp.where(mask, s.astype(jnp.float32), -jnp.inf)
    m = jnp.max(s, axis=-1, keepdims=True)
    m = jnp.where(jnp.isfinite(m), m, 0.0)
    e = jnp.exp(s - m)
    return e / jnp.maximum(jnp.sum(e, axis=-1, keepdims=True), 1e-30)


def sink_softmax(s, mask, sink):
    s = jnp.where(mask, s.astype(jnp.float32), -jnp.inf)
    m = jnp.maximum(jnp.max(s, axis=-1, keepdims=True), sink)
    e = jnp.exp(s - m)
    return e / (jnp.sum(e, axis=-1, keepdims=True) + jnp.exp(sink - m))


def swa_sink_attention(q, k, v, sinks, tab):
    B, T = q.shape[0], q.shape[1]
    G, R = A_KV_HEADS, A_Q_HEADS // A_KV_HEADS
    nb = T // Q_BLOCK
    n_prev = -(-A_WINDOW // Q_BLOCK)
    n_keys = (n_prev + 1) * Q_BLOCK

    def band(t):
        tp = jnp.pad(t, ((0, 0), (n_prev * Q_BLOCK, 0), (0, 0), (0, 0)))
        parts = [tp[:, j * Q_BLOCK:j * Q_BLOCK + T].reshape(B, nb, Q_BLOCK, G, HEAD_DIM) for j in range(n_prev + 1)]
        return jnp.concatenate(parts, axis=2)

    kb, vb = band(k), band(v)
    qb = q.reshape(B, nb, Q_BLOCK, G, R, HEAD_DIM)
    kk = jnp.arange(n_keys)
    dist = jnp.arange(Q_BLOCK)[:, None] + n_prev * Q_BLOCK - kk[None, :]
    kpos = jnp.arange(nb)[:, None] * Q_BLOCK - n_prev * Q_BLOCK + kk[None, :]
    mask = ((dist >= 0) & (dist < A_WINDOW))[None] & (kpos >= 0)[:, None, :]
    s = jnp.einsum('bnqgrd,bnkgd->bgrnqk', qb, kb) * ATTN_SCALE
    s = s.astype(jnp.float32) + head_bias(tab, dist, G, R)[:, :, None]
    p = sink_softmax(s, mask, sinks.astype(jnp.float32).reshape(G, R, 1, 1, 1))
    o = jnp.einsum('bgrnqk,bnkgd->bnqgrd', p.astype(v.dtype), vb)
    return o.reshape(B, T, A_Q_HEADS * HEAD_DIM)


def nsa_compress(kv, pe, w1, w2):
    T = kv.shape[1]
    n_cmp = (T - NSA_CMP_LEN) // NSA_CMP_STRIDE + 1
    idx = (jnp.arange(n_cmp) * NSA_CMP_STRIDE)[:, None] + jnp.arange(NSA_CMP_LEN)[None, :]
    blocks = kv[:, idx] + pe[None, None, :, None, :]
    hid = jax.nn.gelu(jnp.einsum('bnlgd,ldf->bngf', blocks, w1.reshape(NSA_CMP_LEN, HEAD_DIM, NSA_CMP_HIDDEN)))
    return jnp.einsum('bngf,fd->bngd', hid, w2)


def nsa_attention(q, k_c, v_c, k_s, v_s, k_w, v_w, gates, tab):
    B, T = q.shape[0], q.shape[1]
    G, R = B_KV_HEADS, B_Q_HEADS // B_KV_HEADS
    nb = T // Q_BLOCK
    NC = k_c.shape[1]
    NS = T // NSA_SEL_LEN
    top_n = min(NSA_TOP_N, NS)
    cstart = jnp.arange(NC) * NSA_CMP_STRIDE
    cend = cstart + NSA_CMP_LEN - 1
    sstart = jnp.arange(NS) * NSA_SEL_LEN
    overlap = ((cstart[:, None] < sstart[None, :] + NSA_SEL_LEN) & (cstart[:, None] + NSA_CMP_LEN > sstart[None, :])).astype(jnp.float32)
    tab_g = jnp.transpose(tab.reshape(REL_BUCKETS, G, R), (1, 0, 2))
    g_ids = jnp.arange(G)[None, :, None, None]
    ks_blk = jnp.transpose(k_s.reshape(B, NS, NSA_SEL_LEN, G, HEAD_DIM), (0, 3, 1, 2, 4))
    vs_blk = jnp.transpose(v_s.reshape(B, NS, NSA_SEL_LEN, G, HEAD_DIM), (0, 3, 1, 2, 4))
    kw_pad = jnp.pad(k_w, ((0, 0), (NSA_WINDOW, 0), (0, 0), (0, 0)))
    vw_pad = jnp.pad(v_w, ((0, 0), (NSA_WINDOW, 0), (0, 0), (0, 0)))
    take_blocks = jax.vmap(jax.vmap(lambda blk, ix: blk[ix]))
    q_blocks = jnp.transpose(q.reshape(B, nb, Q_BLOCK, G, R, HEAD_DIM), (1, 0, 2, 3, 4, 5))
    g_blocks = jnp.transpose(gates.reshape(B, nb, Q_BLOCK, G, R, 3), (1, 0, 2, 3, 4, 5))
    blk_ids = jnp.arange(NS)

    def one_block(args):
        i, qb, gb = args
        qpos = i * Q_BLOCK + jnp.arange(Q_BLOCK)
        dc = qpos[:, None] - cend[None, :]
        s_c = jnp.einsum('bqgrd,bcgd->bgrqc', qb, k_c) * ATTN_SCALE
        p_c = masked_softmax(s_c.astype(jnp.float32) + head_bias(tab, dc, G, R), dc >= 0)
        o_c = jnp.einsum('bgrqc,bcgd->bqgrd', p_c.astype(v_c.dtype), v_c)
        imp = jnp.einsum('bgrqc,cs->bgqs', p_c, overlap)
        cur = qpos // NSA_SEL_LEN
        forced = (blk_ids[None, :] == 0) | (blk_ids[None, :] == cur[:, None]) | (blk_ids[None, :] == cur[:, None] - 1)
        future = sstart[None, :] > qpos[:, None]
        imp = jnp.where(future, -jnp.inf, jnp.where(forced, NSA_FORCE_SCORE, imp))
        top_val, top_idx = lax.top_k(imp, top_n)
        n_sel = top_n * NSA_SEL_LEN
        ks = take_blocks(ks_blk, top_idx).reshape(B, G, Q_BLOCK, n_sel, HEAD_DIM)
        vs = take_blocks(vs_blk, top_idx).reshape(B, G, Q_BLOCK, n_sel, HEAD_DIM)
        kpos_s = (top_idx[..., None] * NSA_SEL_LEN + jnp.arange(NSA_SEL_LEN)).reshape(B, G, Q_BLOCK, n_sel)
        ds = qpos[None, None, :, None] - kpos_s
        mask_s = jnp.repeat(jnp.isfinite(top_val), NSA_SEL_LEN, axis=-1) & (ds >= 0)
        bias_s = jnp.moveaxis(tab_g[g_ids, rel_bucket(ds)], -1, 2)
        s_s = jnp.einsum('bqgrd,bgqkd->bgrqk', qb, ks) * ATTN_SCALE
        p_s = masked_softmax(s_s.astype(jnp.float32) + bias_s, mask_s[:, :, None])
        o_s = jnp.einsum('bgrqk,bgqkd->bqgrd', p_s.astype(vs.dtype), vs)
        kw = lax.dynamic_slice_in_dim(kw_pad, i * Q_BLOCK, Q_BLOCK + NSA_WINDOW, axis=1)
        vw = lax.dynamic_slice_in_dim(vw_pad, i * Q_BLOCK, Q_BLOCK + NSA_WINDOW, axis=1)
        kpos_w = i * Q_BLOCK - NSA_WINDOW + jnp.arange(Q_BLOCK + NSA_WINDOW)
        dw = qpos[:, None] - kpos_w[None, :]
        mask_w = (dw >= 0) & (dw < NSA_WINDOW) & (kpos_w >= 0)[None, :]
        s_w = jnp.einsum('bqgrd,bkgd->bgrqk', qb, kw) * ATTN_SCALE
        p_w = masked_softmax(s_w.astype(jnp.float32) + head_bias(tab, dw, G, R), mask_w)
        o_w = jnp.einsum('bgrqk,bkgd->bqgrd', p_w.astype(vw.dtype), vw)
        g = jax.nn.sigmoid(gb.astype(jnp.float32)).astype(qb.dtype)
        o = g[..., 0:1] * o_c + g[..., 1:2] * o_s + g[..., 2:3] * o_w
        return o.reshape(B, Q_BLOCK, B_Q_HEADS * HEAD_DIM)

    out = lax.map(one_block, (jnp.arange(nb), q_blocks, g_blocks))
    return jnp.transpose(out, (1, 0, 2, 3)).reshape(B, T, B_Q_HEADS * HEAD_DIM)


def forgetting_attention(q, k, v, f_logit):
    B, T, H = q.shape[0], q.shape[1], q.shape[2]
    nb = T // Q_BLOCK
    c = jnp.moveaxis(jnp.cumsum(jax.nn.log_sigmoid(f_logit.astype(jnp.float32)), axis=1), 1, 2)
    q_blocks = jnp.transpose(q.reshape(B, nb, Q_BLOCK, H, HEAD_DIM), (1, 0, 2, 3, 4))
    c_blocks = jnp.transpose(c.reshape(B, H, nb, Q_BLOCK), (2, 0, 1, 3))
    kpos = jnp.arange(T)

    def one_block(args):
        i, qb, cq = args
        qpos = i * Q_BLOCK + jnp.arange(Q_BLOCK)
        s = jnp.einsum('bqhd,bkhd->bhqk', qb, k).astype(jnp.float32) * ATTN_SCALE
        s = s + cq[..., None] - c[:, :, None, :]
        p = masked_softmax(s, kpos[None, :] <= qpos[:, None])
        return jnp.einsum('bhqk,bkhd->bqhd', p.astype(v.dtype), v)

    out = lax.map(one_block, (jnp.arange(nb), q_blocks, c_blocks))
    return jnp.transpose(out, (1, 0, 2, 3, 4)).reshape(B, T, H * HEAD_DIM)


def even_mixer(h, w_in, w_out, sinks, pe_k, pe_v, ck_w1, ck_w2, cv_w1, cv_w2, rel_bias):
    B, T = h.shape[0], h.shape[1]
    splits = [int(s) for s in np.cumsum(EVEN_SIZES)[:-1]]
    qa, ka, va, qb, kc, vc, ksl, vsl, kwn, vwn, gt = jnp.split(h @ w_in, splits, axis=-1)
    heads = lambda t, n: t.reshape(B, T, n, HEAD_DIM)
    a_out = swa_sink_attention(heads(qa, A_Q_HEADS), heads(ka, A_KV_HEADS), heads(va, A_KV_HEADS), sinks, rel_bias[:, :A_Q_HEADS])
    k_cmp = nsa_compress(heads(kc, B_KV_HEADS), pe_k, ck_w1, ck_w2)
    v_cmp = nsa_compress(heads(vc, B_KV_HEADS), pe_v, cv_w1, cv_w2)
    b_out = nsa_attention(heads(qb, B_Q_HEADS), k_cmp, v_cmp, heads(ksl, B_KV_HEADS), heads(vsl, B_KV_HEADS), heads(kwn, B_KV_HEADS), heads(vwn, B_KV_HEADS), gt, rel_bias[:, A_Q_HEADS:])
    return jnp.concatenate([a_out, b_out], axis=-1) @ w_out


def odd_mixer(h, w_in, w_out, f_bias):
    B, T = h.shape[0], h.shape[1]
    q, k, v, f = jnp.split(h @ w_in, [C_MIX, 2 * C_MIX, 3 * C_MIX], axis=-1)
    heads = lambda t: t.reshape(B, T, C_HEADS, HEAD_DIM)
    return forgetting_attention(heads(q), heads(k), heads(v), f + f_bias) @ w_out


def setup_inputs(seed: int = 0) -> dict:
    key = jax.random.key(seed)
    ks = jax.random.split(key, 20)
    ne = (DEPTH + 1) // 2
    no = DEPTH // 2
    nrm = lambda k, shape, scale: jax.random.normal(k, shape, jnp.float32) * scale
    flat_cmp = NSA_CMP_LEN * HEAD_DIM
    return {
        'x': nrm(ks[0], (BATCH, SEQ, D_MODEL), 1.0),
        'rel_bias': nrm(ks[1], (REL_BUCKETS, REL_HEADS), 0.3),
        'norm_mix': 1.0 + nrm(ks[2], (DEPTH, D_MODEL), 0.05),
        'norm_ffn': 1.0 + nrm(ks[3], (DEPTH, D_MODEL), 0.05),
        'norm_final': 1.0 + nrm(ks[4], (D_MODEL,), 0.05),
        'w_in_even': nrm(ks[5], (ne, D_MODEL, EVEN_IN), D_MODEL ** -0.5),
        'w_out_even': nrm(ks[6], (ne, EVEN_MIX, D_MODEL), EVEN_MIX ** -0.5),
        'a_sinks': nrm(ks[7], (ne, A_Q_HEADS), 0.5),
        'nsa_pe_k': nrm(ks[8], (ne, NSA_CMP_LEN, HEAD_DIM), 0.1),
        'nsa_pe_v': nrm(ks[9], (ne, NSA_CMP_LEN, HEAD_DIM), 0.1),
        'nsa_cmp_k_w1': nrm(ks[10], (ne, flat_cmp, NSA_CMP_HIDDEN), flat_cmp ** -0.5),
        'nsa_cmp_k_w2': nrm(ks[11], (ne, NSA_CMP_HIDDEN, HEAD_DIM), NSA_CMP_HIDDEN ** -0.5),
        'nsa_cmp_v_w1': nrm(ks[12], (ne, flat_cmp, NSA_CMP_HIDDEN), flat_cmp ** -0.5),
        'nsa_cmp_v_w2': nrm(ks[13], (ne, NSA_CMP_HIDDEN, HEAD_DIM), NSA_CMP_HIDDEN ** -0.5),
        'w_in_odd': nrm(ks[14], (no, D_MODEL, ODD_IN), D_MODEL ** -0.5),
        'w_out_odd': nrm(ks[15], (no, C_MIX, D_MODEL), C_MIX ** -0.5),
        'fox_fgate_b': 3.0 + nrm(ks[16], (no, C_HEADS), 0.5),
        'w_ffn_up': nrm(ks[17], (DEPTH, D_MODEL, D_FF), D_MODEL ** -0.5),
        'w_ffn_down': nrm(ks[18], (DEPTH, D_FF, D_MODEL), D_FF ** -0.5),
    }


def reference(x, rel_bias, norm_mix, norm_ffn, norm_final, w_in_even, w_out_even, a_sinks, nsa_pe_k, nsa_pe_v, nsa_cmp_k_w1, nsa_cmp_k_w2, nsa_cmp_v_w1, nsa_cmp_v_w2, w_in_odd, w_out_odd, fox_fgate_b, w_ffn_up, w_ffn_down):
    for layer in range(DEPTH):
        h = rms_norm(x, norm_mix[layer])
        if layer % 2 == 0:
            e = layer // 2
            x = x + even_mixer(h, w_in_even[e], w_out_even[e], a_sinks[e], nsa_pe_k[e], nsa_pe_v[e], nsa_cmp_k_w1[e], nsa_cmp_k_w2[e], nsa_cmp_v_w1[e], nsa_cmp_v_w2[e], rel_bias)
        else:
            o = layer // 2
            x = x + odd_mixer(h, w_in_odd[o], w_out_odd[o], fox_fgate_b[o])
        h = rms_norm(x, norm_ffn[layer])
        u = jax.nn.relu(h @ w_ffn_up[layer])
        x = x + (u * u) @ w_ffn_down[layer]
    return rms_norm(x, norm_final)
```

```python
import numpy as np
import concourse.bass as bass
import concourse.mybir as mybir
from concourse.bass_utils import run_bass_kernel_spmd

F32 = mybir.dt.float32
BF16 = mybir.dt.bfloat16
AF = mybir.ActivationFunctionType
ALU = mybir.AluOpType
AX = mybir.AxisListType


class Res:
    __slots__ = ("w", "r", "name")

    def __init__(self, name=""):
        self.w = None
        self.r = []
        self.name = name


class _Ctr:
    def __init__(self, sem, name):
        self.sem = sem
        self.n = 0
        self.name = name


class Q:
    def __init__(self, prog, name, eng, sem, same_wait=True):
        self.p = prog
        self.name = name
        self.eng = eng
        self.ctr = _Ctr(sem, name)
        self.seen = {}
        self.pend_r = []
        self.pend_w = []
        self.same_wait = same_wait
        self.dma_slots = []
        self.dma_i = 0

    def _wait(self, ctr, val):
        if ctr is self.ctr and not self.same_wait:
            return
        if self.seen.get(ctr, 0) >= val:
            return
        self.eng.wait_ge(ctr.sem, val)
        self.seen[ctr] = val

    def _deps(self, reads, writes):
        for r in reads:
            if r.w is not None:
                self._wait(*r.w)
        for w in writes:
            if w.w is not None:
                self._wait(*w.w)
            for rr in w.r:
                self._wait(*rr)

    def _record(self, reads, writes, tag):
        for r in reads:
            r.r.append(tag)
        for w in writes:
            w.w = tag
            w.r = []

    def op(self, fn, reads=(), writes=(), inc=True):
        self._deps(reads, writes)
        ins = fn(self.eng)
        if inc:
            self.ctr.n += 1
            ins.then_inc(self.ctr.sem, 1)
            tag = (self.ctr, self.ctr.n)
            self._record(list(reads) + self.pend_r, list(writes) + self.pend_w, tag)
            self.pend_r, self.pend_w = [], []
        else:
            self.pend_r += list(reads)
            self.pend_w += list(writes)
        return ins

    def dma(self, out, in_, reads=(), writes=(), **kw):
        self._deps(reads, writes)
        slot = self.dma_slots[self.dma_i % len(self.dma_slots)]
        self.dma_i += 1
        if slot.n > 0:
            self._wait(slot, slot.n)
        ins = self.eng.dma_start(out=out, in_=in_, **kw)
        slot.n += 16
        ins.then_inc(slot.sem, 16)
        tag = (slot, slot.n)
        self._record(reads, writes, tag)
        return tag

    def wait_tag(self, tag):
        self._wait(*tag)


class Prog:
    def __init__(self, name="k"):
        self.nc = bass.Bass("TRN2", target_bir_lowering=False, name=name)
        self.stack = None
        self._all_dma_tags = []

    def start(self, stack, n_dma_slots=8, dma_queues=("sp", "act", "pool")):
        nc = self.nc
        self.stack = stack
        E = stack.enter_context

        def mk(name, eng, same_wait=True):
            sem = E(nc.semaphore(f"s_{name}"))
            return Q(self, name, eng, sem, same_wait)

        self.pe = mk("pe", nc.tensor, same_wait=False)
        self.act = mk("act", nc.scalar)
        self.dve = mk("dve", nc.vector)
        self.pool = mk("pool", nc.gpsimd)
        self.sp = mk("sp", nc.sync)
        for q in (self.sp, self.act, self.pool):
            if q.name in dma_queues:
                q.dma_slots = [_Ctr(E(nc.semaphore(f"d_{q.name}{i}")), f"d_{q.name}{i}")
                               for i in range(n_dma_slots)]
        self.queues = [self.pe, self.act, self.dve, self.pool, self.sp]

    pfx = ""

    def sb(self, name, shape, dt):
        return self.stack.enter_context(self.nc.sbuf_tensor(self.pfx + name, list(shape), dt))

    def ps(self, name, shape, dt=F32):
        return self.stack.enter_context(self.nc.psum_tensor(self.pfx + name, list(shape), dt))

    def finish(self, out_tags):
        for t in out_tags:
            self.sp.wait_tag(t)
        for q in (self.pe, self.act, self.dve, self.pool):
            if q.ctr.n > 0:
                self.sp._wait(q.ctr, q.ctr.n)


def run(nc, in_maps, n_cores=None, trace=False):
    n = len(in_maps)
    res = run_bass_kernel_spmd(nc, in_maps, core_ids=list(range(n)), trace=trace)
    return res


class Ring:
    def __init__(self, items):
        self.items = items
        self.i = 0

    def next(self):
        it = self.items[self.i % len(self.items)]
        self.i += 1
        return it


D_MODEL = 2048
RMS_EPS = 1e-6


def make_ident(p, dt=BF16, name="ident"):
    ident = p.sb(name, [128, 128], dt)
    rid = Res(name)
    p.pool.op(lambda e: e.memset(ident[:], 0.0), writes=[rid])
    p.pool.op(lambda e: e.affine_select(out=ident[:], in_=ident[:], pattern=[[-1, 128]],
                                        compare_op=ALU.not_equal, fill=1.0, base=0,
                                        channel_multiplier=1), reads=[rid], writes=[rid])
    return ident, rid


def norm_transpose(p, xt, rx, hT, rhT, tok0, gT, rg, ident, rid, scr):
    junk, rjunk, ss, rss, hb, rhb, ptr = scr
    D = D_MODEL
    p.act.op(lambda e: e.activation(out=junk[:], in_=xt[:], func=AF.Square, accum_out=ss[:]),
             reads=[rx], writes=[rjunk, rss])
    p.dve.op(lambda e: e.tensor_scalar(out=ss[:], in0=ss[:], scalar1=1.0 / D, scalar2=RMS_EPS,
                                       op0=ALU.mult, op1=ALU.add), reads=[rss], writes=[rss])
    p.act.op(lambda e: e.sqrt(out=ss[:], in_=ss[:]), reads=[rss], writes=[rss])
    p.dve.op(lambda e: e.reciprocal(out=ss[:], in_=ss[:]), reads=[rss], writes=[rss])
    p.dve.op(lambda e: e.tensor_scalar(out=hb[:], in0=xt[:], scalar1=ss[:, 0:1], scalar2=None,
                                       op0=ALU.mult), reads=[rx, rss], writes=[rhb])
    for c4 in range(4):
        pt, rpt = ptr.next()
        for j in range(4):
            c = c4 * 4 + j
            p.pe.op(lambda e: e.transpose(pt[:, j * 128:(j + 1) * 128], hb[:, c * 128:(c + 1) * 128], ident[:]),
                    reads=[rhb, rid], writes=[rpt], inc=(j == 3))
        for j in range(4):
            c = c4 * 4 + j
            q = p.act if (j % 2 == 0) else p.dve
            if q is p.act:
                q.op(lambda e: e.activation(out=hT[:, c, tok0:tok0 + 128], in_=pt[:, j * 128:(j + 1) * 128],
                                            func=AF.Copy, scale=gT[:, c:c + 1]),
                     reads=[rpt, rg], writes=[rhT])
            else:
                q.op(lambda e: e.tensor_scalar(out=hT[:, c, tok0:tok0 + 128], in0=pt[:, j * 128:(j + 1) * 128],
                                               scalar1=gT[:, c:c + 1], scalar2=None, op0=ALU.mult),
                     reads=[rpt, rg], writes=[rhT])


def build_proj(N_in, fm_ranges, tm_ranges, T=2048, name="proj", host=None):
    import contextlib
    NT = T // 128
    n_fm = sum(b - a for a, b in fm_ranges)
    n_tm = sum(b - a for a, b in tm_ranges)
    if host is None:
        p = Prog(name)
        nc = p.nc
        x = nc.dram_tensor("x", [T, D_MODEL], F32, kind="ExternalInput").ap()
        g = nc.dram_tensor("gT", [128, 16], F32, kind="ExternalInput").ap()
        w = nc.dram_tensor("w", [D_MODEL, N_in], F32, kind="ExternalInput").ap()
        outT = nc.dram_tensor("outT", [n_fm, T], F32, kind="ExternalOutput").ap()
        outV = nc.dram_tensor("outV", [T, max(n_tm, 1)], F32, kind="ExternalOutput").ap()
    else:
        p, a = host
        nc = p.nc
        x, g, w, outT, outV = a["x"], a["gT"], a["w"], a["outT"], a["outV"]
    wv = w.rearrange("(c p) n -> p c n", p=128)
    out_tags = []
    with contextlib.ExitStack() as st:
        if host is None:
            p.start(st)
        else:
            p.stack = st
        ident, rid = make_ident(p)
        gT = p.sb("gTs", [128, 16], F32); rg = Res("g")
        p.sp.dma(gT[:], g[:, :], writes=[rg])
        hT = p.sb("hT", [128, 16, T], BF16); rhT = Res("hT")
        xr = Ring([(p.sb(f"x{i}", [128, D_MODEL], F32), Res(f"x{i}")) for i in range(3)])
        junk = p.sb("junk", [128, D_MODEL], BF16); rjunk = Res()
        ssr = Ring([(p.sb(f"ss{i}", [128, 1], F32), Res()) for i in range(2)])
        hbr = Ring([(p.sb(f"hb{i}", [128, D_MODEL], BF16), Res()) for i in range(2)])
        ptr = Ring([(p.ps(f"pt{i}", [128, 512], BF16), Res()) for i in range(2)])
        for t in range(NT):
            xt, rx = xr.next()
            p.sp.dma(xt[:], x[t * 128:(t + 1) * 128, :], writes=[rx])
            ss, rss = ssr.next()
            hb, rhb = hbr.next()
            norm_transpose(p, xt, rx, hT, rhT, t * 128, gT, rg, ident, rid,
                           (junk, rjunk, ss, rss, hb, rhb, ptr))
        blocks = []
        row = 0
        for a, b in fm_ranges:
            c0 = a
            while c0 < b:
                c1 = min(c0 + 512, b)
                blocks.append(("fm", c0, c1, row))
                row += c1 - c0
                c0 = c1
        col = 0
        for a, b in tm_ranges:
            c0 = a
            while c0 < b:
                c1 = min(c0 + 512, b)
                blocks.append(("tm", c0, c1, col))
                col += c1 - c0
                c0 = c1
        wr = Ring([(p.sb(f"w{i}", [128, 16, 512], BF16), Res(f"w{i}")) for i in range(2)])
        psr = Ring([(p.ps(f"py{i}", [128, 512], F32), Res()) for i in range(4)])
        stg = Ring([(p.sb(f"stg{i}", [128, T], F32), Res()) for i in range(2)])
        stv = Ring([(p.sb(f"stv{i}", [128, 512], F32), Res()) for i in range(3)])
        ev = 0
        for kind, c0, c1, o0 in blocks:
            wb, rw = wr.next()
            nb = c1 - c0
            p.pool.dma(wb[:, :, 0:nb], wv[:, :, c0:c1], writes=[rw])
            if kind == "fm":
                for j0 in range(0, nb, 128):
                    wj = min(128, nb - j0)
                    sg, rsg = stg.next()
                    for tg in range(T // 512):
                        py, rpy = psr.next()
                        for c in range(16):
                            p.pe.op(lambda e: e.matmul(py[0:wj, :], wb[:, c, j0:j0 + wj], hT[:, c, tg * 512:(tg + 1) * 512],
                                                       start=(c == 0), stop=(c == 15)),
                                    reads=[rw, rhT], writes=[rpy], inc=(c == 15))
                        ev += 1
                        if ev % 2:
                            p.act.op(lambda e: e.copy(out=sg[0:wj, tg * 512:(tg + 1) * 512], in_=py[0:wj, :]),
                                     reads=[rpy], writes=[rsg])
                        else:
                            p.dve.op(lambda e: e.tensor_copy(sg[0:wj, tg * 512:(tg + 1) * 512], py[0:wj, :]),
                                     reads=[rpy], writes=[rsg])
                    out_tags.append(p.sp.dma(outT[o0 + j0:o0 + j0 + wj, :], sg[0:wj, :], reads=[rsg]))
            else:
                for t in range(NT):
                    py, rpy = psr.next()
                    for c in range(16):
                        p.pe.op(lambda e: e.matmul(py[:, 0:nb], hT[:, c, t * 128:(t + 1) * 128], wb[:, c, 0:nb],
                                                   start=(c == 0), stop=(c == 15)),
                                reads=[rw, rhT], writes=[rpy], inc=(c == 15))
                    sv, rsv = stv.next()
                    ev += 1
                    if ev % 2:
                        p.act.op(lambda e: e.copy(out=sv[:, 0:nb], in_=py[:, 0:nb]), reads=[rpy], writes=[rsv])
                    else:
                        p.dve.op(lambda e: e.tensor_copy(sv[:, 0:nb], py[:, 0:nb]), reads=[rpy], writes=[rsv])
                    out_tags.append(p.sp.dma(outV[t * 128:(t + 1) * 128, o0:o0 + nb], sv[:, 0:nb], reads=[rsv]))
        if host is None:
            p.finish(out_tags)
        else:
            drain(p)
    return nc


def build_post(final, T=2048, TH=1024, DFF=8192, name="post", host=None):
    import contextlib
    D = D_MODEL
    if host is None:
        p = Prog(name)
        nc = p.nc
        x = nc.dram_tensor("x", [T, D], F32, kind="ExternalInput").ap()
        OT = nc.dram_tensor("OT", [D, T], F32, kind="ExternalInput").ap()
        wo = nc.dram_tensor("wo", [D, D], F32, kind="ExternalInput").ap()
        g = nc.dram_tensor("gT", [128, 16], F32, kind="ExternalInput").ap()
        wup = nc.dram_tensor("wup", [D, DFF], F32, kind="ExternalInput").ap()
        wdn = nc.dram_tensor("wdn", [DFF, D], F32, kind="ExternalInput").ap()
        if final:
            gf = nc.dram_tensor("gf", [128, D], F32, kind="ExternalInput").ap()
        xo = nc.dram_tensor("xo", [T, D], F32, kind="ExternalOutput").ap()
    else:
        p, a = host
        nc = p.nc
        x, OT, wo, g, wup, wdn, xo = a["x"], a["OT"], a["wo"], a["gT"], a["wup"], a["wdn"], a["xo"]
        if final:
            gf = a["gf"]
    OTv = OT.rearrange("(c p) t -> p c t", p=128)
    wov = wo.rearrange("(c p) n -> p c n", p=128)
    wupv = wup.rearrange("(c p) n -> p c n", p=128)
    wdnv = wdn.rearrange("(j p) n -> p j n", p=128)
    NTH = TH // 128
    FB = 256
    out_tags = []
    with contextlib.ExitStack() as st:
        if host is None:
            p.start(st)
        else:
            p.stack = st
        ident, rid = make_ident(p)
        gT = p.sb("gTs", [128, 16], F32); rg = Res("g")
        p.sp.dma(gT[:], g[:, :], writes=[rg])
        if final:
            gfs = p.sb("gfs", [128, D], F32); rgf = Res("gf")
            p.sp.dma(gfs[:], gf[:, :], writes=[rgf])
        actT = p.sb("actT", [128, 16, TH], BF16); ract = Res("actT")
        xs = [(p.sb(f"xs{i}", [128, D], F32), Res(f"xs{i}")) for i in range(NTH)]
        junk = p.sb("junk", [128, D], BF16); rjunk = Res()
        ssr = Ring([(p.sb(f"ss{i}", [128, 1], F32), Res()) for i in range(2)])
        hbr = Ring([(p.sb(f"hb{i}", [128, D], BF16), Res()) for i in range(2)])
        ptr = Ring([(p.ps(f"pt{i}", [128, 512], BF16), Res()) for i in range(2)])
        wur = Ring([(p.sb(f"wu{i}", [128, 16, FB], BF16), Res(f"wu{i}")) for i in range(2)])
        wdr = Ring([(p.sb(f"wd{i}", [128, FB // 128, D], BF16), Res(f"wd{i}")) for i in range(2)])
        uTr = Ring([(p.sb(f"uT{i}", [128, FB // 128, TH], BF16), Res(f"uT{i}")) for i in range(2)])
        rsr = Ring([(p.sb(f"rs{i}", [128, 512], F32), Res()) for i in range(2)])
        pyr = Ring([(p.ps(f"py{i}", [128, 512], F32), Res()) for i in range(3)])
        pur = Ring([(p.ps(f"pu{i}", [128, 512], F32), Res()) for i in range(2)])
        for hh in range(T // TH):
            t0 = hh * TH
            p.pool.dma(actT[:, :, :], OTv[:, :, t0:t0 + TH], writes=[ract])
            for t in range(NTH):
                xt, rx = xs[t]
                p.sp.dma(xt[:], x[t0 + t * 128:t0 + (t + 1) * 128, :], writes=[rx])
            for nb in range(D // FB):
                wb, rw = wur.next()
                p.pool.dma(wb[:, :, :], wov[:, :, nb * FB:(nb + 1) * FB], writes=[rw])
                for t in range(NTH):
                    xt, rx = xs[t]
                    py, rpy = pyr.next()
                    for c in range(16):
                        p.pe.op(lambda e: e.matmul(py[:, 0:FB], actT[:, c, t * 128:(t + 1) * 128], wb[:, c, :],
                                                   start=(c == 0), stop=(c == 15)),
                                reads=[ract, rw], writes=[rpy], inc=(c == 15))
                    p.dve.op(lambda e: e.tensor_tensor(out=xt[:, nb * FB:(nb + 1) * FB], in0=xt[:, nb * FB:(nb + 1) * FB],
                                                       in1=py[:, 0:FB], op=ALU.add),
                             reads=[rpy, rx], writes=[rx])
            for t in range(NTH):
                xt, rx = xs[t]
                ss, rss = ssr.next()
                hb, rhb = hbr.next()
                norm_transpose(p, xt, rx, actT, ract, t * 128, gT, rg, ident, rid,
                               (junk, rjunk, ss, rss, hb, rhb, ptr))
            for fb in range(DFF // FB):
                wu, rwu = wur.next()
                p.pool.dma(wu[:, :, :], wupv[:, :, fb * FB:(fb + 1) * FB], writes=[rwu])
                wd, rwd = wdr.next()
                p.pool.dma(wd[:, :, :], wdnv[:, fb * (FB // 128):(fb + 1) * (FB // 128), :], writes=[rwd])
                uT, ruT = uTr.next()
                for j in range(FB // 128):
                    for tg in range(TH // 512):
                        pu, rpu = pur.next()
                        for c in range(16):
                            p.pe.op(lambda e: e.matmul(pu[:, :], wu[:, c, j * 128:(j + 1) * 128],
                                                       actT[:, c, tg * 512:(tg + 1) * 512],
                                                       start=(c == 0), stop=(c == 15)),
                                    reads=[ract, rwu], writes=[rpu], inc=(c == 15))
                        rs, rrs = rsr.next()
                        p.act.op(lambda e: e.activation(out=rs[:], in_=pu[:], func=AF.Relu),
                                 reads=[rpu], writes=[rrs])
                        p.pool.op(lambda e: e.tensor_tensor(out=uT[:, j, tg * 512:(tg + 1) * 512], in0=rs[:], in1=rs[:],
                                                            op=ALU.mult), reads=[rrs], writes=[ruT])
                for t in range(NTH):
                    xt, rx = xs[t]
                    for nb in range(D // 512):
                        py, rpy = pyr.next()
                        nj = FB // 128
                        for j in range(nj):
                            p.pe.op(lambda e: e.matmul(py[:, :], uT[:, j, t * 128:(t + 1) * 128],
                                                       wd[:, j, nb * 512:(nb + 1) * 512],
                                                       start=(j == 0), stop=(j == nj - 1)),
                                    reads=[ruT, rwd], writes=[rpy], inc=(j == nj - 1))
                        p.dve.op(lambda e: e.tensor_tensor(out=xt[:, nb * 512:(nb + 1) * 512],
                                                           in0=xt[:, nb * 512:(nb + 1) * 512],
                                                           in1=py[:, :], op=ALU.add),
                                 reads=[rpy, rx], writes=[rx])
            for t in range(NTH):
                xt, rx = xs[t]
                if final:
                    ss, rss = ssr.next()
                    p.act.op(lambda e: e.activation(out=junk[:], in_=xt[:], func=AF.Square, accum_out=ss[:]),
                             reads=[rx], writes=[rjunk, rss])
                    p.dve.op(lambda e: e.tensor_scalar(out=ss[:], in0=ss[:], scalar1=1.0 / D, scalar2=RMS_EPS,
                                                       op0=ALU.mult, op1=ALU.add), reads=[rss], writes=[rss])
                    p.act.op(lambda e: e.sqrt(out=ss[:], in_=ss[:]), reads=[rss], writes=[rss])
                    p.dve.op(lambda e: e.reciprocal(out=ss[:], in_=ss[:]), reads=[rss], writes=[rss])
                    p.dve.op(lambda e: e.tensor_scalar(out=xt[:], in0=xt[:], scalar1=ss[:, 0:1], scalar2=None,
                                                       op0=ALU.mult), reads=[rx, rss], writes=[rx])
                    p.dve.op(lambda e: e.tensor_tensor(out=xt[:], in0=xt[:], in1=gfs[:], op=ALU.mult),
                             reads=[rx, rgf], writes=[rx])
                out_tags.append(p.sp.dma(xo[t0 + t * 128:t0 + (t + 1) * 128, :], xt[:], reads=[rx]))
        if host is None:
            p.finish(out_tags)
        else:
            if final:
                host[1]["out_tags"] += out_tags
            drain(p)
    return nc


class AttnCtx:
    def __init__(self, p, n_s=3, n_pt=4):
        self.p = p
        self.sr = Ring([(p.ps(f"psS{i}", [128, 512], F32), Res(f"psS{i}")) for i in range(n_s)])
        self.ptr = Ring([(p.sb(f"pT{i}", [128, 512], BF16), Res(f"pT{i}")) for i in range(n_pt)])


def run_jobs(ctx, jobs, lookahead=2, late=2):
    p = ctx.p
    n = len(jobs)
    staged = {}
    pend_late = []

    def emit_s(i):
        jb = jobs[i]
        if "pre" in jb:
            jb["pre"]()
        if "pre2" in jb:
            jb["pre2"]()
        ps, rps = ctx.sr.next()
        kp = jb["kp"]
        mm = jb["s_mms"]
        for k, (lhsT, rhs, reads) in enumerate(mm):
            p.pe.op(lambda e: e.matmul(ps[0:kp, :], lhsT, rhs, start=(k == 0), stop=(k == len(mm) - 1)),
                    reads=reads, writes=[rps], inc=(k == len(mm) - 1))
        pt, rpt = ctx.ptr.next()
        b = jb.get("bias")
        if b is None:
            p.act.op(lambda e: e.activation(out=pt[0:kp, :], in_=ps[0:kp, :], func=AF.Exp, scale=jb["scale"]),
                     reads=[rps], writes=[rpt])
        else:
            p.act.op(lambda e: e.activation(out=pt[0:kp, :], in_=ps[0:kp, :], func=AF.Exp, scale=jb["scale"],
                                            bias=b[0]), reads=[rps] + list(b[1]), writes=[rpt])
        staged[i] = (pt, rpt)

    def emit_pv(i):
        jb = jobs[i]
        pt, rpt = staged.pop(i)
        kp = jb["kp"]
        for (out_ap, lhsT, rout, reads, start, stop) in jb["pv"]:
            p.pe.op(lambda e: e.matmul(out_ap, lhsT, pt[0:kp, :], start=start, stop=stop),
                    reads=[rpt] + list(reads), writes=[rout], inc=stop)
        d = jb.get("done")
        if d is not None:
            lt = d()
            if lt is not None:
                pend_late.append([late, lt])

    def tick_late(force=False):
        for it in list(pend_late):
            it[0] -= 1
            if it[0] <= 0 or force:
                it[1]()
                pend_late.remove(it)

    for i in range(min(lookahead, n)):
        emit_s(i)
    for i in range(n):
        if i + lookahead < n:
            emit_s(i + lookahead)
        emit_pv(i)
        tick_late()
    while pend_late:
        tick_late(force=True)


NEG = -240000.0


def causal_masks():
    k = np.arange(128)[:, None, None]
    j = np.arange(4)[None, :, None]
    q = np.arange(512)[None, None, :]
    return np.where(128 * j + k <= q, 0.0, NEG).astype(np.float32)


def make_normalizer(p, n_bc=1):
    ones = p.sb("ones65", [65, 64], F32)
    rones = Res("ones65")
    p.pool.op(lambda e: e.memset(ones[:], 1.0), writes=[rones])
    bcr = Ring([(p.ps(f"pbc{i}", [64, 512], F32), Res(f"pbc{i}")) for i in range(n_bc)])
    rdr = Ring([(p.sb(f"rden{i}", [65, 512], F32), Res(f"rden{i}")) for i in range(3)])
    osr = Ring([(p.sb(f"osb{i}", [64, 512], F32), Res(f"osb{i}")) for i in range(3)])
    return dict(ones=ones, rones=rones, bcr=bcr, rdr=rdr, osr=osr)


def build_fox(NH, T=4096, name="fox", host=None):
    import contextlib
    if host is None:
        p = Prog(name)
        nc = p.nc
        qT = nc.dram_tensor("qT", [NH * 64, T], F32, kind="ExternalInput").ap()
        kT = nc.dram_tensor("kT", [NH * 64, T], F32, kind="ExternalInput").ap()
        v = nc.dram_tensor("v", [T, NH * 64], F32, kind="ExternalInput").ap()
        fT = nc.dram_tensor("fT", [NH, T], F32, kind="ExternalInput").ap()
        fb = nc.dram_tensor("fb", [NH, 1], F32, kind="ExternalInput").ap()
        mk = nc.dram_tensor("masks", [128, 4, 512], F32, kind="ExternalInput").ap()
        OT = nc.dram_tensor("OT", [NH * 64, T], F32, kind="ExternalOutput").ap()
    else:
        p, a = host
        nc = p.nc
        qT, kT, v, fT, fb, mk, OT = a["qT"], a["kT"], a["v"], a["fT"], a["fb"], a["masks"], a["OT"]
    NB = T // 128
    NQC = T // 512
    vv = v.rearrange("(n p) c -> p n c", p=128)
    out_tags = []
    with contextlib.ExitStack() as st:
        if host is None:
            p.start(st)
        else:
            p.stack = st
        ident, rid = make_ident(p)
        nz = make_normalizer(p)
        masks = p.sb("masks_sb", [128, 4, 512], BF16); rmask = Res("masks")
        p.pool.dma(masks[:], mk[:, :, :], writes=[rmask])
        fl = p.sb("fl", [NH, T], F32); rfl = Res("fl")
        p.sp.dma(fl[:], fT[:, :], writes=[rfl])
        fbs = p.sb("fbs", [NH, 1], F32); rfb = Res("fb")
        p.sp.dma(fbs[:], fb[:, :], writes=[rfb])
        p.dve.op(lambda e: e.tensor_scalar(out=fbs[:], in0=fbs[:], scalar1=-1.0, scalar2=None, op0=ALU.mult),
                 reads=[rfb], writes=[rfb])
        p.act.op(lambda e: e.activation(out=fl[:], in_=fl[:], func=AF.Exp, scale=-1.0, bias=fbs[:, 0:1]),
                 reads=[rfl, rfb], writes=[rfl])
        p.act.op(lambda e: e.activation(out=fl[:], in_=fl[:], func=AF.Ln, bias=1.0, scale=1.0),
                 reads=[rfl], writes=[rfl])
        zer = p.sb("zer", [NH, T], F32); rz = Res("zer")
        p.pool.op(lambda e: e.memset(zer[:], 0.0), writes=[rz])
        p.dve.op(lambda e: e.tensor_scalar(out=fl[:], in0=fl[:], scalar1=-8.0, scalar2=None, op0=ALU.mult),
                 reads=[rfl], writes=[rfl])
        c8 = p.sb("c8", [NH, T], F32); rc8 = Res("c8")
        p.dve.op(lambda e: e.tensor_tensor_scan(out=c8[:], data0=fl[:], data1=zer[:], initial=0.0,
                                                op0=ALU.add, op1=ALU.add), reads=[rfl, rz], writes=[rc8])
        cq = p.sb("cq", [NH, 3, T], BF16); rcq = Res("cq")
        ck = p.sb("ck", [NH, 3, T], BF16); rck = Res("ck")
        for i in range(3):
            p.dve.op(lambda e: e.tensor_copy(cq[:, i, :], c8[:]), reads=[rc8], writes=[rcq])
            if i < 2:
                p.dve.op(lambda e: e.tensor_tensor(out=c8[:], in0=c8[:], in1=cq[:, i, :], op=ALU.subtract),
                         reads=[rc8, rcq], writes=[rc8])
            p.dve.op(lambda e: e.tensor_scalar(out=ck[:, i, :], in0=cq[:, i, :], scalar1=-1.0, scalar2=None,
                                               op0=ALU.mult), reads=[rcq], writes=[rck])
        Qr = Ring([(p.sb(f"Qa{i}", [70, T], BF16), Res(f"Qa{i}")) for i in range(2)])
        Kr = Ring([(p.sb(f"Ka{i}", [70, T], BF16), Res(f"Ka{i}")) for i in range(2)])
        Vr = Ring([(p.sb(f"Va{i}", [128, NB, 65], BF16), Res(f"Va{i}")) for i in range(2)])
        for (t_, r_) in Qr.items + Kr.items:
            p.pool.op(lambda e: e.memset(t_[64:70, :], 1.0), writes=[r_])
        for (t_, r_) in Vr.items:
            p.pool.op(lambda e: e.memset(t_[:, :, 64:65], 1.0), writes=[r_])
        por = Ring([(p.ps(f"po{i}", [65, 512], F32), Res(f"po{i}")) for i in range(2)])
        ohr = Ring([(p.sb(f"oh{i}", [64, T], F32), Res(f"oh{i}")) for i in range(2)])
        ctx = AttnCtx(p)
        jobs = []
        for h in range(NH):
            Qa, rQ = Qr.next(); Ka, rK = Kr.next(); Va, rV = Vr.next(); oh, roh = ohr.next()

            def load(h=h, Qa=Qa, rQ=rQ, Ka=Ka, rK=rK, Va=Va, rV=rV):
                p.pool.dma(Qa[0:64, :], qT[h * 64:(h + 1) * 64, :], writes=[rQ])
                p.pool.dma(Ka[0:64, :], kT[h * 64:(h + 1) * 64, :], writes=[rK])
                p.pool.dma(Va[:, :, 0:64], vv[:, :, h * 64:(h + 1) * 64], writes=[rV])
                for i in range(3):
                    p.sp.dma(Qa[64 + i:65 + i, :], cq[h:h + 1, i, :], reads=[rcq], writes=[rQ])
                    p.sp.dma(Ka[67 + i:68 + i, :], ck[h:h + 1, i, :], reads=[rck], writes=[rK])
            first = True
            for qc in range(NQC):
                po, rpo = por.next()
                nkb = 4 * qc + 4
                for kb in range(nkb):
                    mm = [(Ka[0:70, kb * 128:(kb + 1) * 128], Qa[0:70, qc * 512:(qc + 1) * 512], [rQ, rK])]
                    if kb >= 4 * qc:
                        mm.append((ident[:], masks[:, kb - 4 * qc, :], [rid, rmask]))
                    jb = dict(s_mms=mm, kp=128, scale=0.125,
                              pv=[(po[0:65, :], Va[:, kb, 0:65], rpo, [rV], kb == 0, kb == nkb - 1)])
                    if first:
                        jb["pre"] = load
                        first = False
                    if kb == nkb - 1:
                        jb["done"] = make_norm_done(p, nz, po, rpo, oh, roh, qc,
                                                    final=(lambda h=h, oh=oh, roh=roh: out_tags.append(
                                                        p.sp.dma(OT[h * 64:(h + 1) * 64, :], oh[:], reads=[roh])))
                                                    if qc == NQC - 1 else None)
                    jobs.append(jb)
        run_jobs(ctx, jobs)
        if host is None:
            p.finish(out_tags)
        else:
            drain(p)
    return nc


def make_norm_done(p, nz, po, rpo, oh, roh, qc, final=None, sink=None):
    def done():
        rd, rrd = nz["rdr"].next()
        osb, ros = nz["osr"].next()
        if sink is not None:
            es, res_ = sink
            p.dve.op(lambda e: e.tensor_scalar(out=rd[64:65, :], in0=po[64:65, :], scalar1=es, scalar2=None,
                                               op0=ALU.add), reads=[rpo, res_], writes=[rrd])
            p.dve.op(lambda e: e.reciprocal(out=rd[64:65, :], in_=rd[64:65, :]), reads=[rrd], writes=[rrd])
        else:
            p.dve.op(lambda e: e.reciprocal(out=rd[64:65, :], in_=po[64:65, :]), reads=[rpo], writes=[rrd])
        p.act.op(lambda e: e.copy(out=osb[:], in_=po[0:64, :]), reads=[rpo], writes=[ros])

        def late():
            bc, rbc = nz["bcr"].next()
            p.pe.op(lambda e: e.matmul(bc[:, :], nz["ones"][64:65, 0:64], rd[64:65, :], start=True, stop=True),
                    reads=[nz["rones"], rrd], writes=[rbc])
            p.dve.op(lambda e: e.tensor_tensor(out=oh[:, qc * 512:(qc + 1) * 512], in0=osb[:], in1=bc[:, :],
                                               op=ALU.mult), reads=[ros, rbc], writes=[roh])
            if final is not None:
                final()
        return late
    return done


HD = 64
NSA_W1 = 2048
MASKV = -30000.0


def rel_bucket_np(dist):
    import math
    n = np.maximum(dist, 0)
    nf = np.maximum(n, 1).astype(np.float32)
    large = 16 + (np.log(nf / np.float32(16)) / np.float32(math.log(1024 / 16)) * np.float32(16)).astype(np.int32)
    return np.where(n < 16, n, np.minimum(large, 31))


def skew_bias(tabcol, r_max, width, lo, hi):
    ki = np.arange(128)[:, None]
    X = np.arange(width)[None, :]
    dist = X - ki - 128 * r_max
    val = tabcol[rel_bucket_np(dist)]
    return np.where((dist >= lo) & (dist < hi), val, np.float32(MASKV)).astype(np.float32)


def cmp_bias(tabcol):
    ci = np.arange(128)[:, None]
    X = np.arange(4096)[None, :]
    dc = X - 16 * ci - 31
    val = tabcol[rel_bucket_np(dc)]
    return np.where(dc >= 0, val, np.float32(MASKV)).astype(np.float32)


def nsa_consts():
    c = np.arange(256)
    s = np.arange(64)
    cstart = c * 16
    ov = ((cstart[:, None] < s[None, :] * 64 + 64) & (cstart[:, None] + 32 > s[None, :] * 64)).astype(np.float32)
    ov[255, :] = 0.0
    ovl = np.ascontiguousarray(ov.reshape(2, 128, 64).transpose(1, 0, 2))
    esel = (np.arange(4096)[None, :] // 64 == s[:, None]).astype(np.float32)
    q = np.arange(4096)
    cur = q // 64
    forced = (s[None, :] == 0) | (s[None, :] == cur[:, None]) | (s[None, :] == cur[:, None] - 1)
    future = (s[None, :] * 64) > q[:, None]
    A = np.where(future | forced, 0.0, 1.0).astype(np.float32)
    Bm = np.where(future, -1e30, np.where(forced, 1e4, 0.0)).astype(np.float32)
    tkA = np.ascontiguousarray(A.reshape(32, 128, 64).transpose(1, 0, 2))
    tkB = np.ascontiguousarray(Bm.reshape(32, 128, 64).transpose(1, 0, 2))
    selg = np.zeros((24, 24, 65), np.float32)
    for r in range(24):
        selg[r, r, :] = 1.0
    return dict(ovl=ovl, esel=esel, tkA=tkA, tkB=tkB, selg=selg.reshape(24, 24 * 65))


def build_even(T=4096, do_swa=True, do_nsa=True, name="even", NH=8, host=None):
    import contextlib
    NB = T // 128
    NQC = T // 512
    if host is None:
        p = Prog(name)
        nc = p.nc

        def din(nm, shape):
            return nc.dram_tensor(nm, list(shape), F32, kind="ExternalInput").ap()
        OT = nc.dram_tensor("OT", [2 * NH * 64, T], F32, kind="ExternalOutput").ap()
        OT_nsa = OT[NH * 64:2 * NH * 64, :]
    else:
        p, a = host
        nc = p.nc

        def din(nm, shape):
            ap = a[nm]
            assert list(ap.shape) == list(shape), (nm, ap.shape, shape)
            return ap
        OT = a["OT_swa"]
        OT_nsa = a["OT_nsa"]
    out_tags = []
    with contextlib.ExitStack() as st:
        if host is None:
            p.start(st)
        else:
            p.stack = st
        ident, rid = make_ident(p)
        ident8 = p.sb("ident8", [128, 128], BF16); rid8 = Res("ident8")
        p.dve.op(lambda e: e.tensor_scalar(out=ident8[:], in0=ident[:], scalar1=8.0, scalar2=None, op0=ALU.mult),
                 reads=[rid], writes=[rid8])
        nz = make_normalizer(p)
        ctx = AttnCtx(p)
        por = Ring([(p.ps(f"po{i}", [65, 512], F32), Res(f"po{i}")) for i in range(2)])
        if do_swa:
            sqT = din("sqT", [NH * 64, T]); skT = din("skT", [128, T]); sv = din("sv", [T, 128])
            sinks = din("sinks", [1, NH]); Wswa = din("Wswa", [NH, 128, 1024])
            with contextlib.ExitStack() as st2:
                p.stack = st2
                KT = p.sb("sKT", [64, 2, T], BF16); rKT = Res("sKT")
                p.pool.dma(KT[:], skT.rearrange("(g d) t -> d g t", d=64), writes=[rKT])
                Va = p.sb("sVa", [128, 2, NB, 65], BF16); rVa = Res("sVa")
                p.pool.op(lambda e: e.memset(Va[:, :, :, 64:65], 1.0), writes=[rVa])
                for g in range(2):
                    p.pool.dma(Va[:, g, :, 0:64], sv.rearrange("(n p) c -> p n c", p=128)[:, :, g * 64:(g + 1) * 64],
                               writes=[rVa])
                es = p.sb("esink", [65, NH], F32); res_ = Res("esink")
                p.sp.dma(es[64:65, :], sinks[:, :], writes=[res_])
                p.act.op(lambda e: e.activation(out=es[64:65, :], in_=es[64:65, :], func=AF.Exp),
                         reads=[res_], writes=[res_])
                Qr = Ring([(p.sb(f"sQ{i}", [64, T], BF16), Res(f"sQ{i}")) for i in range(2)])
                Wr = Ring([(p.sb(f"sW{i}", [128, 1024], BF16), Res(f"sW{i}")) for i in range(2)])
                ohr = Ring([(p.sb(f"soh{i}", [64, T], F32), Res(f"soh{i}")) for i in range(2)])
                jobs = []
                for h in range(NH):
                    g = h // 4
                    Q, rQ = Qr.next(); W, rW = Wr.next(); oh, roh = ohr.next()

                    def load(h=h, Q=Q, rQ=rQ, W=W, rW=rW):
                        p.pool.dma(Q[:, :], sqT[h * 64:(h + 1) * 64, :], writes=[rQ])
                        p.pool.dma(W[:, :], Wswa[h, :, :], writes=[rW])
                    first = True
                    for qc in range(NQC):
                        po, rpo = por.next()
                        kbs = list(range(max(0, 4 * qc - 1), 4 * qc + 4))
                        for kb in kbs:
                            r = kb - 4 * qc
                            mm = [(KT[:, g, kb * 128:(kb + 1) * 128], Q[:, qc * 512:(qc + 1) * 512], [rKT, rQ]),
                                  (ident8[:], W[:, 128 * (3 - r):128 * (3 - r) + 512], [rid8, rW])]
                            jb = dict(s_mms=mm, kp=128, scale=0.125,
                                      pv=[(po[0:65, :], Va[:, g, kb, 0:65], rpo, [rVa], kb == kbs[0], kb == kbs[-1])])
                            if first:
                                jb["pre"] = load
                                first = False
                            if kb == kbs[-1]:
                                fin = None
                                if qc == NQC - 1:
                                    fin = (lambda h=h, oh=oh, roh=roh: out_tags.append(
                                        p.sp.dma(OT[h * 64:(h + 1) * 64, :], oh[:], reads=[roh])))
                                jb["done"] = make_norm_done(p, nz, po, rpo, oh, roh, qc, final=fin,
                                                            sink=(es[64:65, h:h + 1], res_))
                            jobs.append(jb)
                run_jobs(ctx, jobs)
                drain(p)
            p.stack = st
        if do_nsa:
            build_nsa_part(p, st, din, OT_nsa, out_tags, ident, rid, ident8, rid8, nz, ctx, por, T, NH)
        if host is None:
            p.finish(out_tags)
        else:
            drain(p)
    return nc


def drain(p):
    qs = [p.pe, p.act, p.dve, p.pool, p.sp]
    for q in qs:
        for o in qs:
            if o is not q and o.ctr.n > 0:
                q._wait(o.ctr, o.ctr.n)
        for o in qs:
            for sl in o.dma_slots:
                if sl.n > 0:
                    q._wait(sl, sl.n)


def make_gated_done(p, nz, S, po, rpo, h, br, gs, rgs, first, acc_ap, racc, cmp_extra=None, final=None):
    NG = S["NG"]

    def done():
        rd, rrd = nz["rdr"].next()
        osb, ros = nz["osr"].next()
        p.dve.op(lambda e: e.tensor_scalar(out=rd[64:65, :], in0=po[64:65, :], scalar1=1e-30, scalar2=None,
                                           op0=ALU.max), reads=[rpo], writes=[rrd])
        p.dve.op(lambda e: e.reciprocal(out=rd[64:65, :], in_=rd[64:65, :]), reads=[rrd], writes=[rrd])
        p.act.op(lambda e: e.copy(out=osb[:], in_=po[0:64, :]), reads=[rpo], writes=[ros])
        if cmp_extra is not None:
            pu, rpu, imp, rimp = cmp_extra
            pus, rpus = S["pusr"].next()
            p.act.op(lambda e: e.copy(out=pus[:], in_=pu[0:64, :]), reads=[rpu], writes=[rpus])

        def late():
            gbc, rgbc = S["gbc"]
            r = 3 * h + br
            p.pe.op(lambda e: e.matmul(gbc[0:65, :], S["selg"][0:NG, r * 65:(r + 1) * 65], gs[0:NG, :],
                                       start=True, stop=True), reads=[S["rselg"], rgs], writes=[rgbc])
            rd2, rrd2 = S["rd2r"].next()
            p.dve.op(lambda e: e.tensor_tensor(out=rd2[64:65, :], in0=rd[64:65, :], in1=gbc[64:65, :], op=ALU.mult),
                     reads=[rrd, rgbc], writes=[rrd2])
            bc, rbc = nz["bcr"].next()
            p.pe.op(lambda e: e.matmul(bc[:, :], nz["ones"][64:65, 0:64], rd2[64:65, :], start=True, stop=True),
                    reads=[nz["rones"], rrd2], writes=[rbc])
            if first:
                p.dve.op(lambda e: e.tensor_tensor(out=acc_ap, in0=osb[:], in1=bc[:, :], op=ALU.mult),
                         reads=[ros, rbc], writes=[racc])
            else:
                tmp, rtmp = S["tmpr"].next()
                p.dve.op(lambda e: e.tensor_tensor(out=tmp[:], in0=osb[:], in1=bc[:, :], op=ALU.mult),
                         reads=[ros, rbc], writes=[rtmp])
                p.dve.op(lambda e: e.tensor_tensor(out=acc_ap, in0=acc_ap, in1=tmp[:], op=ALU.add),
                         reads=[rtmp, racc], writes=[racc])
            if cmp_extra is not None:
                p.pe.op(lambda e: e.matmul(gbc[0:64, :], nz["ones"][64:65, 0:64], rd[64:65, :], start=True, stop=True),
                        reads=[nz["rones"], rrd], writes=[rgbc])
                if h == 0:
                    p.dve.op(lambda e: e.tensor_tensor(out=imp[:], in0=pus[:], in1=gbc[0:64, :], op=ALU.mult),
                             reads=[rpus, rgbc], writes=[rimp])
                else:
                    tmp2, rtmp2 = S["tmpr"].next()
                    p.dve.op(lambda e: e.tensor_tensor(out=tmp2[:], in0=pus[:], in1=gbc[0:64, :], op=ALU.mult),
                             reads=[rpus, rgbc], writes=[rtmp2])
                    p.dve.op(lambda e: e.tensor_tensor(out=imp[:], in0=imp[:], in1=tmp2[:], op=ALU.add),
                             reads=[rtmp2, rimp], writes=[rimp])
            if final is not None:
                final()
        return late
    return done


def build_nsa_part(p, st, din, OT, out_tags, ident, rid, ident8, rid8, nz, ctx, por, T, NH):
    import contextlib
    NB = T // 128
    NQC = T // 512
    NG = 3 * NH
    nqT = din("nqT", [NH * 64, T]); kcT = din("kcT", [64, T]); vcT = din("vcT", [64, T])
    ksT = din("ksT", [64, T]); vs = din("vs", [T, 64]); kwT = din("kwT", [64, T]); vw = din("vw", [T, 64])
    gTd = din("gT", [NG, T]); pekT = din("pekT", [64, 32]); pevT = din("pevT", [64, 32])
    w1k = din("w1k", [2048, 256]); w2k = din("w2k", [256, 64]); w1v = din("w1v", [2048, 256]); w2v = din("w2v", [256, 64])
    Wc = din("Wc", [NH, 128, 4096]); Wsel = din("Wsel", [NH, 128, 2048]); Wwin = din("Wwin", [NH, 128, 1408])
    ovl_d = din("ovl", [128, 2, 64]); esel_d = din("esel", [64, 4096])
    tkA_d = din("tkA", [128, 32, 64]); tkB_d = din("tkB", [128, 32, 64]); selg_d = din("selg", [24, 24 * 65])
    nqv = nqT.rearrange("(h d) t -> d h t", d=64)
    OTn = OT.rearrange("(h d) t -> d h t", d=64)

    kcmpT = p.sb("kcmpT", [64, 256], BF16); rkc = Res("kcmpT")
    vcmpa = p.sb("vcmpa", [128, 2, 65], BF16); rvc = Res("vcmpa")
    p.pool.op(lambda e: e.memset(kcmpT[:], 0.0), writes=[rkc])
    p.pool.op(lambda e: e.memset(vcmpa[:, :, 64:65], 1.0), writes=[rvc])
    with contextlib.ExitStack() as st3:
        p.stack = st3
        for which, xd, w1d, w2d, ped in (("k", kcT, w1k, w2k, pekT), ("v", vcT, w1v, w2v, pevT)):
            xT = p.sb(f"cx{which}", [64, T], BF16); rx = Res()
            p.pool.dma(xT[:], xd[:, :], writes=[rx])
            w1 = p.sb(f"cw1{which}", [64, 32, 256], BF16); rw1 = Res()
            p.pool.dma(w1[:], w1d.rearrange("(l d) f -> d l f", d=64), writes=[rw1])
            w2 = p.sb(f"cw2{which}", [128, 2, 64], BF16); rw2 = Res()
            p.pool.dma(w2[:], w2d.rearrange("(c p) d -> p c d", p=128), writes=[rw2])
            peT = p.sb(f"cpe{which}", [64, 32], BF16); rpe = Res()
            p.pool.dma(peT[:], ped[:, :], writes=[rpe])
            hid = p.sb(f"chid{which}", [128, 2, 256], BF16); rhid = Res()
            p.pool.op(lambda e: e.memset(hid[:], 0.0), writes=[rhid])
            bsb = p.sb(f"cb{which}", [128, 1], F32); rbsb = Res()
            xx = p.sb(f"cxx{which}", [128, 256], F32); rxx = Res()
            tt = p.sb(f"ctt{which}", [128, 256], F32); rtt = Res()
            for fc in range(2):
                ph, rph = ctx.sr.next()
                for l in range(32):
                    p.pe.op(lambda e: e.matmul(ph[:, 0:255], w1[:, l, fc * 128:(fc + 1) * 128],
                                               xT[:, l:l + 16 * 254 + 1:16], start=(l == 0), stop=(l == 31)),
                            reads=[rw1, rx], writes=[rph], inc=(l == 31))
                pb, rpb = ctx.sr.next()
                for l in range(32):
                    p.pe.op(lambda e: e.matmul(pb[:, 0:1], w1[:, l, fc * 128:(fc + 1) * 128], peT[:, l:l + 1],
                                               start=(l == 0), stop=(l == 31)),
                            reads=[rw1, rpe], writes=[rpb], inc=(l == 31))
                p.act.op(lambda e: e.copy(out=bsb[:], in_=pb[:, 0:1]), reads=[rpb], writes=[rbsb])
                p.dve.op(lambda e: e.tensor_scalar(out=xx[:, 0:255], in0=ph[:, 0:255], scalar1=bsb[:, 0:1], scalar2=None,
                                                   op0=ALU.add), reads=[rph, rbsb], writes=[rxx])
                p.dve.op(lambda e: e.tensor_tensor(out=tt[:, 0:255], in0=xx[:, 0:255], in1=xx[:, 0:255], op=ALU.mult),
                         reads=[rxx], writes=[rtt])
                p.dve.op(lambda e: e.tensor_scalar(out=tt[:, 0:255], in0=tt[:, 0:255], scalar1=0.044715, scalar2=1.0,
                                                   op0=ALU.mult, op1=ALU.add), reads=[rtt], writes=[rtt])
                p.dve.op(lambda e: e.tensor_tensor(out=tt[:, 0:255], in0=tt[:, 0:255], in1=xx[:, 0:255], op=ALU.mult),
                         reads=[rtt, rxx], writes=[rtt])
                p.act.op(lambda e: e.activation(out=tt[:, 0:255], in_=tt[:, 0:255], func=AF.Sigmoid,
                                                scale=1.5957691216057308), reads=[rtt], writes=[rtt])
                p.dve.op(lambda e: e.tensor_tensor(out=hid[:, fc, 0:255], in0=xx[:, 0:255], in1=tt[:, 0:255], op=ALU.mult),
                         reads=[rtt, rxx], writes=[rhid])
            if which == "k":
                pk, rpk = ctx.sr.next()
                for fc in range(2):
                    p.pe.op(lambda e: e.matmul(pk[0:64, 0:255], w2[:, fc, :], hid[:, fc, 0:255],
                                               start=(fc == 0), stop=(fc == 1)),
                            reads=[rw2, rhid], writes=[rpk], inc=(fc == 1))
                p.act.op(lambda e: e.copy(out=kcmpT[:, 0:255], in_=pk[0:64, 0:255]), reads=[rpk], writes=[rkc])
            else:
                for cc in range(2):
                    pv, rpv = ctx.sr.next()
                    for fc in range(2):
                        p.pe.op(lambda e: e.matmul(pv[:, 0:64], hid[:, fc, cc * 128:(cc + 1) * 128], w2[:, fc, :],
                                                   start=(fc == 0), stop=(fc == 1)),
                                reads=[rw2, rhid], writes=[rpv], inc=(fc == 1))
                    p.act.op(lambda e: e.copy(out=vcmpa[:, cc, 0:64], in_=pv[:, 0:64]), reads=[rpv], writes=[rvc])
        drain(p)
    p.stack = st
    ovl = p.sb("ovl_sb", [128, 2, 64], BF16); rovl = Res("ovl")
    p.pool.dma(ovl[:], ovl_d[:, :, :], writes=[rovl])
    esel = p.sb("esel_sb", [64, 4096], BF16); resel = Res("esel")
    p.pool.dma(esel[:], esel_d[:, :], writes=[resel])
    selg = p.sb("selg_sb", [24, 24 * 65], F32); rselg = Res("selg")
    p.sp.dma(selg[:], selg_d[:, :], writes=[rselg])
    identf, ridf = make_ident(p, F32, "identf")
    KsT = p.sb("KsT", [64, T], BF16); rKs = Res("KsT")
    p.pool.dma(KsT[:], ksT[:, :], writes=[rKs])
    KwT = p.sb("KwT", [64, T], BF16); rKw = Res("KwT")
    p.pool.dma(KwT[:], kwT[:, :], writes=[rKw])
    Vsa = p.sb("Vsa", [128, NB, 65], BF16); rVs = Res("Vsa")
    Vwa = p.sb("Vwa", [128, NB, 65], BF16); rVw = Res("Vwa")
    for (V_, rV_, vd) in ((Vsa, rVs, vs), (Vwa, rVw, vw)):
        p.pool.op(lambda e: e.memset(V_[:, :, 64:65], 1.0), writes=[rV_])
        p.pool.dma(V_[:, :, 0:64], vd.rearrange("(n p) c -> p n c", p=128), writes=[rV_])
    Qr = Ring([(p.sb(f"nQ{i}", [64, NH, 512], BF16), Res(f"nQ{i}")) for i in range(2)])
    Bc = Ring([(p.sb(f"nBc{i}", [128, 2, 512], BF16), Res(f"nBc{i}")) for i in range(4)])
    Bw = Ring([(p.sb(f"nBw{i}", [128, 1408], BF16), Res(f"nBw{i}")) for i in range(3)])
    Bs = Ring([(p.sb(f"nBs{i}", [128, 2048], BF16), Res(f"nBs{i}")) for i in range(3)])
    gsr = Ring([(p.sb(f"ngs{i}", [NG, 512], F32), Res(f"ngs{i}")) for i in range(2)])
    ABr = Ring([(p.sb(f"nA{i}", [128, 256], F32), Res(f"nA{i}"), p.sb(f"nB{i}", [128, 256], F32), Res(f"nB{i}"))
                for i in range(2)])
    acc = p.sb("nacc", [64, NH, 512], F32)
    raccs = [Res(f"nacc{h}") for h in range(NH)]
    impr = Ring([(p.sb(f"nimp{i}", [64, 512], F32), Res(f"nimp{i}")) for i in range(2)])
    MbTr = Ring([(p.sb(f"nMbT{i}", [64, 512], BF16), Res(f"nMbT{i}")) for i in range(2)])
    pu = p.ps("npu", [64, 512], F32); rpu = Res("npu")
    gbc = p.ps("ngbc", [65, 512], F32); rgbc = Res("ngbc")
    S = dict(NG=NG, selg=selg, rselg=rselg, gbc=(gbc, rgbc),
             pusr=Ring([(p.sb(f"npus{i}", [64, 512], F32), Res()) for i in range(2)]),
             tmpr=Ring([(p.sb(f"ntmp{i}", [64, 512], F32), Res()) for i in range(3)]),
             rd2r=Ring([(p.sb(f"nrd2{i}", [65, 512], F32), Res()) for i in range(3)]))
    impm = p.sb("tk_impm", [128, 256], F32); rimpm = Res()
    work = p.sb("tk_work", [128, 256], F32); rwork = Res()
    t1 = p.sb("tk_t1", [128, 256], F32); rt1 = Res()
    Mb = p.sb("tk_Mb", [128, 256], F32); rMb = Res()
    m1 = p.sb("tk_m1", [128, 8], F32); rm1 = Res()
    m2 = p.sb("tk_m2", [128, 8], F32); rm2 = Res()
    thr = p.sb("tk_thr", [128, 1], F32); rthr = Res()

    groups = []
    for qc in range(NQC):
        Q, rQ = Qr.next(); gs, rgs = gsr.next(); A, rA, B, rB = ABr.next()
        imp, rimp = impr.next(); MbT, rMbT = MbTr.next()

        def load_qc(qc=qc, Q=Q, rQ=rQ, gs=gs, rgs=rgs, A=A, rA=rA, B=B, rB=rB):
            p.pool.dma(Q[:, :, :], nqv[:, :, qc * 512:(qc + 1) * 512], writes=[rQ])
            p.sp.dma(gs[:, :], gTd[:, qc * 512:(qc + 1) * 512], writes=[rgs])
            p.act.op(lambda e: e.activation(out=gs[:, :], in_=gs[:, :], func=AF.Sigmoid), reads=[rgs], writes=[rgs])
            p.sp.dma(A[:, :], tkA_d[:, 4 * qc:4 * qc + 4, :].rearrange("p a s -> p (a s)"), writes=[rA])
            p.sp.dma(B[:, :], tkB_d[:, 4 * qc:4 * qc + 4, :].rearrange("p a s -> p (a s)"), writes=[rB])

        def topk1(qc=qc, imp=imp, rimp=rimp, A=A, rA=rA, B=B, rB=rB):
            pt, rpt = ctx.sr.next()
            for qb in range(4):
                p.pe.op(lambda e: e.transpose(pt[:, qb * 64:(qb + 1) * 64], imp[:, qb * 128:(qb + 1) * 128],
                                              identf[0:64, 0:64]), reads=[rimp, ridf], writes=[rpt], inc=(qb == 3))
            p.dve.op(lambda e: e.tensor_tensor(out=impm[:], in0=pt[:, 0:256], in1=A[:, :], op=ALU.mult),
                     reads=[rpt, rA], writes=[rimpm])
            p.dve.op(lambda e: e.tensor_tensor(out=impm[:], in0=impm[:], in1=B[:, :], op=ALU.add),
                     reads=[rimpm, rB], writes=[rimpm])
            for qb in range(4):
                sl = slice(qb * 64, (qb + 1) * 64)
                p.dve.op(lambda e: e.max(out=m1[:, :], in_=impm[:, sl]), reads=[rimpm], writes=[rm1])
                p.dve.op(lambda e: e.match_replace(out=work[:, sl], in_to_replace=m1[:, :], in_values=impm[:, sl],
                                                   imm_value=-3.0e38), reads=[rimpm, rm1], writes=[rwork])
                p.dve.op(lambda e: e.max(out=m2[:, :], in_=work[:, sl]), reads=[rwork], writes=[rm2])
                p.dve.op(lambda e: e.tensor_reduce(out=thr[:, :], in_=m2[:, :], axis=AX.X, op=ALU.min),
                         reads=[rm2], writes=[rthr])
                p.dve.op(lambda e: e.tensor_scalar(out=t1[:, sl], in0=impm[:, sl], scalar1=thr[:, 0:1], scalar2=None,
                                                   op0=ALU.is_ge), reads=[rimpm, rthr], writes=[rt1])
            p.dve.op(lambda e: e.scalar_tensor_tensor(out=t1[:], in0=impm[:], scalar=-1.0e29, in1=t1[:],
                                                      op0=ALU.is_gt, op1=ALU.mult), reads=[rimpm, rt1], writes=[rt1])
            p.dve.op(lambda e: e.tensor_scalar(out=Mb[:], in0=t1[:], scalar1=1.0, scalar2=-NEG,
                                               op0=ALU.subtract, op1=ALU.mult), reads=[rt1], writes=[rMb])

        def topk2(MbT=MbT, rMbT=rMbT):
            pt2, rpt2 = ctx.sr.next()
            for qb in range(4):
                p.pe.op(lambda e: e.transpose(pt2[0:64, qb * 128:(qb + 1) * 128], Mb[:, qb * 64:(qb + 1) * 64],
                                              identf[:, :]), reads=[rMb, ridf], writes=[rpt2], inc=(qb == 3))
            p.act.op(lambda e: e.copy(out=MbT[:, :], in_=pt2[0:64, :]), reads=[rpt2], writes=[rMbT])

        for h in range(NH):
            bc_, rbc_ = Bc.next()
            ccs = [0] if qc < 4 else [0, 1]

            def loads(h=h, qc=qc, bc_=bc_, rbc_=rbc_, ccs=ccs, first=(h == 0), lq=load_qc):
                if first:
                    lq()
                for cc in ccs:
                    m = qc - 4 * cc
                    p.pool.dma(bc_[:, cc, :], Wc[h, :, 512 * m:512 * m + 512], writes=[rbc_])
            po, rpo = por.next()
            jobs = []
            for cc in ccs:
                mm = [(kcmpT[:, cc * 128:(cc + 1) * 128], Q[:, h, :], [rkc, rQ]),
                      (ident8[:], bc_[:, cc, :], [rid8, rbc_])]
                jb = dict(s_mms=mm, kp=128, scale=0.125,
                          pv=[(po[0:65, :], vcmpa[:, cc, 0:65], rpo, [rvc], cc == 0, cc == ccs[-1]),
                              (pu[0:64, :], ovl[:, cc, :], rpu, [rovl], cc == 0, cc == ccs[-1])])
                if cc == ccs[-1]:
                    jb["done"] = make_gated_done(p, nz, S, po, rpo, h, 0, gs, rgs, True, acc[:, h, :], raccs[h],
                                                 cmp_extra=(pu, rpu, imp, rimp))
                jobs.append(jb)
            groups.append(dict(loads=loads, jobs=jobs))
        for h in range(NH):
            bw_, rbw_ = Bw.next()

            def loads(h=h, bw_=bw_, rbw_=rbw_):
                p.pool.dma(bw_[:, :], Wwin[h, :, :], writes=[rbw_])
            po, rpo = por.next()
            kbs = list(range(max(0, 4 * qc - 4), 4 * qc + 4))
            jobs = []
            for kb in kbs:
                r = kb - 4 * qc
                mm = [(KwT[:, kb * 128:(kb + 1) * 128], Q[:, h, :], [rKw, rQ]),
                      (ident8[:], bw_[:, 128 * (3 - r):128 * (3 - r) + 512], [rid8, rbw_])]
                jb = dict(s_mms=mm, kp=128, scale=0.125,
                          pv=[(po[0:65, :], Vwa[:, kb, 0:65], rpo, [rVw], kb == kbs[0], kb == kbs[-1])])
                if kb == kbs[-1]:
                    jb["done"] = make_gated_done(p, nz, S, po, rpo, h, 2, gs, rgs, False, acc[:, h, :], raccs[h])
                jobs.append(jb)
            if h == 1:
                jobs[0]["pre2"] = topk1
            if h == 4:
                jobs[0]["pre2"] = topk2
            groups.append(dict(loads=loads, jobs=jobs))
        for h in range(NH):
            bs_, rbs_ = Bs.next()

            def loads(h=h, bs_=bs_, rbs_=rbs_):
                p.pool.dma(bs_[:, :], Wsel[h, :, :], writes=[rbs_])
            po, rpo = por.next()
            kbs = list(range(0, 4 * qc + 4))
            jobs = []
            for kb in kbs:
                r = max(kb - 4 * qc, -9)
                mm = [(KsT[:, kb * 128:(kb + 1) * 128], Q[:, h, :], [rKs, rQ]),
                      (esel[:, kb * 128:(kb + 1) * 128], MbT[:, :], [resel, rMbT]),
                      (ident8[:], bs_[:, 128 * (3 - r):128 * (3 - r) + 512], [rid8, rbs_])]
                jb = dict(s_mms=mm, kp=128, scale=0.125,
                          pv=[(po[0:65, :], Vsa[:, kb, 0:65], rpo, [rVs], kb == kbs[0], kb == kbs[-1])])
                if kb == kbs[-1]:
                    fin = (lambda h=h, qc=qc: out_tags.append(
                        p.sp.dma(OTn[:, h, qc * 512:(qc + 1) * 512], acc[:, h, :], reads=[raccs[h]])))
                    jb["done"] = make_gated_done(p, nz, S, po, rpo, h, 1, gs, rgs, False, acc[:, h, :], raccs[h],
                                                 final=fin)
                jobs.append(jb)
            groups.append(dict(loads=loads, jobs=jobs))
    PD = 2
    flat = []
    for gi, g in enumerate(groups):
        if gi + PD < len(groups):
            prev = g["jobs"][0].get("pre")
            nxt = groups[gi + PD]["loads"]
            g["jobs"][0]["pre"] = (lambda a=prev, b=nxt: ((a() if a else None), b()))
        flat += g["jobs"]
    for gi in range(min(PD, len(groups))):
        groups[gi]["loads"]()
    run_jobs(ctx, flat)


_PROGS = {}


def _prog(key, fn):
    if key not in _PROGS:
        _PROGS[key] = fn()
    return _PROGS[key]


EVEN_FM = [(0, 1280), (1536, 2944), (3072, 3200), (3328, 3376)]
EVEN_TM = [(1280, 1536), (2944, 3072), (3200, 3328)]
ODD_FM = [(0, 4096), (6144, 6176)]
ODD_TM = [(4096, 6144)]


def _gT(g):
    return np.ascontiguousarray(np.asarray(g, np.float32).reshape(16, 128).T)


def _launch(nc, in_maps):
    res = run_bass_kernel_spmd(nc, in_maps, core_ids=list(range(len(in_maps))))
    return res.results


def kernel_unfused(x, rel_bias, norm_mix, norm_ffn, norm_final, w_in_even, w_out_even, a_sinks, nsa_pe_k, nsa_pe_v,
           nsa_cmp_k_w1, nsa_cmp_k_w2, nsa_cmp_v_w1, nsa_cmp_v_w2, w_in_odd, w_out_odd, fox_fgate_b,
           w_ffn_up, w_ffn_down):
    f32 = lambda a: np.ascontiguousarray(np.asarray(a, dtype=np.float32))
    x = f32(x)
    rel_bias = f32(rel_bias)
    B, T, D = x.shape
    TH = T // 2
    DEPTH = norm_mix.shape[0]
    xs = [np.ascontiguousarray(x[c // 2, (c % 2) * TH:(c % 2 + 1) * TH, :]) for c in range(8)]
    consts = nsa_consts()
    cmask = causal_masks()
    Wswa = [skew_bias(rel_bias[:, h], 3, 1024, 0, 128) for h in range(16)]
    Wc = [cmp_bias(rel_bias[:, 16 + h]) for h in range(16)]
    Wsel = [skew_bias(rel_bias[:, 16 + h], 3, 2048, 0, 1 << 30) for h in range(16)]
    Wwin = [skew_bias(rel_bias[:, 16 + h], 3, 1408, 0, 512) for h in range(16)]
    fm = lambda a: np.ascontiguousarray(a.T)
    for l in range(DEPTH):
        even = (l % 2 == 0)
        i = l // 2
        w_in = f32(w_in_even[i]) if even else f32(w_in_odd[i])
        n_in = w_in.shape[1]
        nc = _prog(("proj", even), lambda: build_proj(n_in, EVEN_FM if even else ODD_FM, EVEN_TM if even else ODD_TM,
                                                      T=TH, name="proj_e" if even else "proj_o"))
        gT = _gT(norm_mix[l])
        res = _launch(nc, [{"x": xs[c], "gT": gT, "w": w_in} for c in range(8)])
        YT = [np.concatenate([res[2 * b]["outT"], res[2 * b + 1]["outT"]], axis=1) for b in range(B)]
        V = [np.concatenate([res[2 * b]["outV"], res[2 * b + 1]["outV"]], axis=0) for b in range(B)]
        del res
        ca = np.ascontiguousarray
        if even:
            nc = _prog("even", lambda: build_even(T=T))
            maps = []
            for c in range(8):
                b, hf = c // 2, c % 2
                Y, Vb = YT[b], V[b]
                m = {"sqT": ca(Y[512 * hf:512 * hf + 512]), "skT": ca(Y[1024 + 128 * hf:1024 + 128 * hf + 128]),
                     "sv": ca(Vb[:, 128 * hf:128 * hf + 128]), "sinks": f32(a_sinks[i][8 * hf:8 * hf + 8]).reshape(1, 8),
                     "Wswa": np.stack(Wswa[8 * hf:8 * hf + 8]),
                     "nqT": ca(Y[1280 + 512 * hf:1280 + 512 * hf + 512]),
                     "kcT": ca(Y[2304 + 64 * hf:2304 + 64 * hf + 64]), "vcT": ca(Y[2432 + 64 * hf:2432 + 64 * hf + 64]),
                     "ksT": ca(Y[2560 + 64 * hf:2560 + 64 * hf + 64]), "vs": ca(Vb[:, 256 + 64 * hf:256 + 64 * hf + 64]),
                     "kwT": ca(Y[2688 + 64 * hf:2688 + 64 * hf + 64]), "vw": ca(Vb[:, 384 + 64 * hf:384 + 64 * hf + 64]),
                     "gT": ca(Y[2816 + 24 * hf:2816 + 24 * hf + 24]),
                     "pekT": fm(f32(nsa_pe_k[i])), "pevT": fm(f32(nsa_pe_v[i])),
                     "w1k": f32(nsa_cmp_k_w1[i]), "w2k": f32(nsa_cmp_k_w2[i]),
                     "w1v": f32(nsa_cmp_v_w1[i]), "w2v": f32(nsa_cmp_v_w2[i]),
                     "Wc": np.stack(Wc[8 * hf:8 * hf + 8]), "Wsel": np.stack(Wsel[8 * hf:8 * hf + 8]),
                     "Wwin": np.stack(Wwin[8 * hf:8 * hf + 8])}
                m.update(consts)
                maps.append(m)
            res = _launch(nc, maps)
            OTb = []
            for b in range(B):
                o = np.empty((2048, T), np.float32)
                for hf in range(2):
                    r = res[2 * b + hf]["OT"]
                    o[512 * hf:512 * hf + 512] = r[0:512]
                    o[1024 + 512 * hf:1024 + 512 * hf + 512] = r[512:1024]
                OTb.append(o)
            w_out = f32(w_out_even[i])
        else:
            nc = _prog("fox", lambda: build_fox(16, T=T))
            maps = []
            for c in range(8):
                b, hf = c // 2, c % 2
                Y, Vb = YT[b], V[b]
                maps.append({"qT": ca(Y[1024 * hf:1024 * hf + 1024]), "kT": ca(Y[2048 + 1024 * hf:2048 + 1024 * hf + 1024]),
                             "v": ca(Vb[:, 1024 * hf:1024 * hf + 1024]), "fT": ca(Y[4096 + 16 * hf:4096 + 16 * hf + 16]),
                             "fb": f32(fox_fgate_b[i][16 * hf:16 * hf + 16]).reshape(16, 1), "masks": cmask})
            res = _launch(nc, maps)
            OTb = [np.concatenate([res[2 * b]["OT"], res[2 * b + 1]["OT"]], axis=0) for b in range(B)]
            w_out = f32(w_out_odd[i])
        del res, YT, V
        final = (l == DEPTH - 1)
        nc = _prog(("post", final), lambda: build_post(final, T=TH, name="post_f" if final else "post"))
        gT = _gT(norm_ffn[l])
        wup = f32(w_ffn_up[l]); wdn = f32(w_ffn_down[l])
        maps = []
        for c in range(8):
            b, th = c // 2, c % 2
            m = {"x": xs[c], "OT": ca(OTb[b][:, th * TH:(th + 1) * TH]), "wo": w_out, "gT": gT, "wup": wup, "wdn": wdn}
            if final:
                m["gf"] = ca(np.broadcast_to(f32(norm_final), (128, D)))
            maps.append(m)
        res = _launch(nc, maps)
        xs = [res[c]["xo"] for c in range(8)]
        del res, OTb
    out = np.empty((B, T, D), np.float32)
    for c in range(8):
        out[c // 2, (c % 2) * TH:(c % 2 + 1) * TH, :] = xs[c]
    return out


def build_mega(T=4096, DEPTH=4, name="mega"):
    import contextlib
    p = Prog(name)
    nc = p.nc
    D = D_MODEL

    def din(nm, shape):
        return nc.dram_tensor(nm, list(shape), F32, kind="ExternalInput").ap()
    x_in = din("x", [T, D])
    gfin = din("gfin", [128, D])
    L = []
    for l in range(DEPTH):
        even = (l % 2 == 0)
        d = dict(gm=din(f"gm{l}", [128, 16]), gf=din(f"gf{l}", [128, 16]),
                 win=din(f"win{l}", [D, 3376 if even else 6176]), wout=din(f"wout{l}", [D, D]),
                 wup=din(f"wup{l}", [D, 4 * D]), wdn=din(f"wdn{l}", [4 * D, D]))
        if even:
            d.update(sinks=din(f"sinks{l}", [1, 16]), pekT=din(f"pekT{l}", [64, 32]), pevT=din(f"pevT{l}", [64, 32]),
                     w1k=din(f"w1k{l}", [2048, 256]), w2k=din(f"w2k{l}", [256, 64]),
                     w1v=din(f"w1v{l}", [2048, 256]), w2v=din(f"w2v{l}", [256, 64]))
        else:
            d.update(fb=din(f"fb{l}", [32, 1]))
        L.append(d)
    Wswa = din("Wswa", [16, 128, 1024]); Wc = din("Wc", [16, 128, 4096])
    Wsel = din("Wsel", [16, 128, 2048]); Wwin = din("Wwin", [16, 128, 1408])
    cst = dict(ovl=din("ovl", [128, 2, 64]), esel=din("esel", [64, 4096]), tkA=din("tkA", [128, 32, 64]),
               tkB=din("tkB", [128, 32, 64]), selg=din("selg", [24, 24 * 65]))
    masks = din("masks", [128, 4, 512])
    out = nc.dram_tensor("out", [T, D], F32, kind="ExternalOutput").ap()
    xbuf = nc.dram_tensor("xbuf", [T, D], F32).ap()
    YT = nc.dram_tensor("YTbuf", [4128, T], F32).ap()
    Vb = nc.dram_tensor("Vbuf", [T, 2048], F32).ap()
    OTb = nc.dram_tensor("OTbuf", [2048, T], F32).ap()
    shared = {"out_tags": []}
    TH = T // 2
    with contextlib.ExitStack() as st:
        p.start(st)
        for l in range(DEPTH):
            even = (l % 2 == 0)
            d = L[l]
            xsrc = x_in if l == 0 else xbuf
            fm, tm = (EVEN_FM, EVEN_TM) if even else (ODD_FM, ODD_TM)
            n_fm = sum(b - a for a, b in fm); n_tm = sum(b - a for a, b in tm)
            for hh in range(2):
                p.pfx = f"L{l}p{hh}_"
                build_proj(3376 if even else 6176, fm, tm, T=TH,
                           host=(p, dict(x=xsrc[hh * TH:(hh + 1) * TH, :], gT=d["gm"], w=d["win"],
                                         outT=YT[0:n_fm, hh * TH:(hh + 1) * TH],
                                         outV=Vb[hh * TH:(hh + 1) * TH, 0:n_tm])))
            if even:
                for hf in range(2):
                    p.pfx = f"L{l}e{hf}_"
                    a = dict(sqT=YT[512 * hf:512 * hf + 512], skT=YT[1024 + 128 * hf:1024 + 128 * hf + 128],
                             sv=Vb[:, 128 * hf:128 * hf + 128], sinks=d["sinks"][:, 8 * hf:8 * hf + 8],
                             Wswa=Wswa[8 * hf:8 * hf + 8], nqT=YT[1280 + 512 * hf:1280 + 512 * hf + 512],
                             kcT=YT[2304 + 64 * hf:2304 + 64 * hf + 64], vcT=YT[2432 + 64 * hf:2432 + 64 * hf + 64],
                             ksT=YT[2560 + 64 * hf:2560 + 64 * hf + 64], vs=Vb[:, 256 + 64 * hf:256 + 64 * hf + 64],
                             kwT=YT[2688 + 64 * hf:2688 + 64 * hf + 64], vw=Vb[:, 384 + 64 * hf:384 + 64 * hf + 64],
                             gT=YT[2816 + 24 * hf:2816 + 24 * hf + 24], pekT=d["pekT"], pevT=d["pevT"],
                             w1k=d["w1k"], w2k=d["w2k"], w1v=d["w1v"], w2v=d["w2v"],
                             Wc=Wc[8 * hf:8 * hf + 8], Wsel=Wsel[8 * hf:8 * hf + 8], Wwin=Wwin[8 * hf:8 * hf + 8],
                             OT_swa=OTb[512 * hf:512 * hf + 512], OT_nsa=OTb[1024 + 512 * hf:1024 + 512 * hf + 512])
                    a.update(cst)
                    build_even(T=T, host=(p, a))
            else:
                p.pfx = f"L{l}f_"
                build_fox(32, T=T, host=(p, dict(qT=YT[0:2048], kT=YT[2048:4096], v=Vb[:, 0:2048], fT=YT[4096:4128],
                                                 fb=d["fb"], masks=masks, OT=OTb[0:2048])))
            final = (l == DEPTH - 1)
            p.pfx = f"L{l}o_"
            a = dict(x=xsrc, OT=OTb, wo=d["wout"], gT=d["gf"], wup=d["wup"], wdn=d["wdn"],
                     xo=(out if final else xbuf), out_tags=shared["out_tags"])
            if final:
                a["gf"] = gfin
            build_post(final, T=T, host=(p, a))
        p.finish(shared["out_tags"])
    return nc


def mega_inputs(inp, DEPTH=4):
    f32 = lambda a: np.ascontiguousarray(np.asarray(a, dtype=np.float32))
    fm = lambda a: np.ascontiguousarray(np.asarray(a, dtype=np.float32).T)
    rel_bias = f32(inp["rel_bias"])
    m = {}
    for l in range(DEPTH):
        i = l // 2
        even = (l % 2 == 0)
        m[f"gm{l}"] = _gT(inp["norm_mix"][l]); m[f"gf{l}"] = _gT(inp["norm_ffn"][l])
        m[f"win{l}"] = f32(inp["w_in_even"][i]) if even else f32(inp["w_in_odd"][i])
        m[f"wout{l}"] = f32(inp["w_out_even"][i]) if even else f32(inp["w_out_odd"][i])
        m[f"wup{l}"] = f32(inp["w_ffn_up"][l]); m[f"wdn{l}"] = f32(inp["w_ffn_down"][l])
        if even:
            m[f"sinks{l}"] = f32(inp["a_sinks"][i]).reshape(1, 16)
            m[f"pekT{l}"] = fm(inp["nsa_pe_k"][i]); m[f"pevT{l}"] = fm(inp["nsa_pe_v"][i])
            m[f"w1k{l}"] = f32(inp["nsa_cmp_k_w1"][i]); m[f"w2k{l}"] = f32(inp["nsa_cmp_k_w2"][i])
            m[f"w1v{l}"] = f32(inp["nsa_cmp_v_w1"][i]); m[f"w2v{l}"] = f32(inp["nsa_cmp_v_w2"][i])
        else:
            m[f"fb{l}"] = f32(inp["fox_fgate_b"][i]).reshape(32, 1)
    m["gfin"] = np.ascontiguousarray(np.broadcast_to(f32(inp["norm_final"]), (128, D_MODEL)))
    m["Wswa"] = np.stack([skew_bias(rel_bias[:, h], 3, 1024, 0, 128) for h in range(16)])
    m["Wc"] = np.stack([cmp_bias(rel_bias[:, 16 + h]) for h in range(16)])
    m["Wsel"] = np.stack([skew_bias(rel_bias[:, 16 + h], 3, 2048, 0, 1 << 30) for h in range(16)])
    m["Wwin"] = np.stack([skew_bias(rel_bias[:, 16 + h], 3, 1408, 0, 512) for h in range(16)])
    m.update(nsa_consts())
    m["masks"] = causal_masks()
    return m


def kernel(**inputs):
    x = np.ascontiguousarray(np.asarray(inputs["x"], dtype=np.float32))
    B, T, D = x.shape
    DEPTH = int(np.asarray(inputs["norm_mix"]).shape[0])
    nc = _prog("mega", lambda: build_mega(T=T, DEPTH=DEPTH))
    shared = mega_inputs(inputs, DEPTH)
    in_maps = []
    for b in range(B):
        m = dict(shared)
        m["x"] = np.ascontiguousarray(x[b])
        in_maps.append(m)
    res = run_bass_kernel_spmd(nc, in_maps, core_ids=list(range(B)))
    return np.stack([res.results[b]["out"] for b in range(B)], axis=0).astype(np.float32)
```

```python
import numpy as np
import concourse.bass as bass
import concourse.mybir as mybir
from concourse.bass_utils import run_bass_kernel_spmd

F32 = mybir.dt.float32
BF16 = mybir.dt.bfloat16
AF = mybir.ActivationFunctionType
ALU = mybir.AluOpType
AX = mybir.AxisListType


class Res:
    __slots__ = ("w", "r", "name")

    def __init__(self, name=""):
        self.w = None
        self.r = []
        self.name = name


class _Ctr:
    def __init__(self, sem, name):
        self.sem = sem
        self.n = 0
        self.name = name


class Q:
    def __init__(self, prog, name, eng, sem, same_wait=True):
        self.p = prog
        self.name = name
        self.eng = eng
        self.ctr = _Ctr(sem, name)
        self.seen = {}
        self.pend_r = []
        self.pend_w = []
        self.same_wait = same_wait
        self.dma_slots = []
        self.dma_i = 0

    def _wait(self, ctr, val):
        if ctr is self.ctr and not self.same_wait:
            return
        if self.seen.get(ctr, 0) >= val:
            return
        self.eng.wait_ge(ctr.sem, val)
        self.seen[ctr] = val

    def _deps(self, reads, writes):
        for r in reads:
            if r.w is not None:
                self._wait(*r.w)
        for w in writes:
            if w.w is not None:
                self._wait(*w.w)
            for rr in w.r:
                self._wait(*rr)

    def _record(self, reads, writes, tag):
        for r in reads:
            r.r.append(tag)
        for w in writes:
            w.w = tag
            w.r = []

    def op(self, fn, reads=(), writes=(), inc=True):
        self._deps(reads, writes)
        ins = fn(self.eng)
        if inc:
            self.ctr.n += 1
            ins.then_inc(self.ctr.sem, 1)
            tag = (self.ctr, self.ctr.n)
            self._record(list(reads) + self.pend_r, list(writes) + self.pend_w, tag)
            self.pend_r, self.pend_w = [], []
        else:
            self.pend_r += list(reads)
            self.pend_w += list(writes)
        return ins

    def dma(self, out, in_, reads=(), writes=(), **kw):
        self._deps(reads, writes)
        slot = self.dma_slots[self.dma_i % len(self.dma_slots)]
        self.dma_i += 1
        if slot.n > 0:
            self._wait(slot, slot.n)
        ins = self.eng.dma_start(out=out, in_=in_, **kw)
        slot.n += 16
        ins.then_inc(slot.sem, 16)
        tag = (slot, slot.n)
        self._record(reads, writes, tag)
        return tag

    def wait_tag(self, tag):
        self._wait(*tag)


class Prog:
    def __init__(self, name="k"):
        self.nc = bass.Bass("TRN2", target_bir_lowering=False, name=name)
        self.stack = None
        self._all_dma_tags = []

    def start(self, stack, n_dma_slots=8, dma_queues=("sp", "act", "pool")):
        nc = self.nc
        self.stack = stack
        E = stack.enter_context

        def mk(name, eng, same_wait=True):
            sem = E(nc.semaphore(f"s_{name}"))
            return Q(self, name, eng, sem, same_wait)

        self.pe = mk("pe", nc.tensor, same_wait=False)
        self.act = mk("act", nc.scalar)
        self.dve = mk("dve", nc.vector)
        self.pool = mk("pool", nc.gpsimd)
        self.sp = mk("sp", nc.sync)
        for q in (self.sp, self.act, self.pool):
            if q.name in dma_queues:
                q.dma_slots = [_Ctr(E(nc.semaphore(f"d_{q.name}{i}")), f"d_{q.name}{i}")
                               for i in range(n_dma_slots)]
        self.queues = [self.pe, self.act, self.dve, self.pool, self.sp]

    pfx = ""

    def sb(self, name, shape, dt):
        return self.stack.enter_context(self.nc.sbuf_tensor(self.pfx + name, list(shape), dt))

    def ps(self, name, shape, dt=F32):
        return self.stack.enter_context(self.nc.psum_tensor(self.pfx + name, list(shape), dt))

    def finish(self, out_tags):
        for t in out_tags:
            self.sp.wait_tag(t)
        for q in (self.pe, self.act, self.dve, self.pool):
            if q.ctr.n > 0:
                self.sp._wait(q.ctr, q.ctr.n)


def run(nc, in_maps, n_cores=None, trace=False):
    n = len(in_maps)
    res = run_bass_kernel_spmd(nc, in_maps, core_ids=list(range(n)), trace=trace)
    return res


class Ring:
    def __init__(self, items):
        self.items = items
        self.i = 0

    def next(self):
        it = self.items[self.i % len(self.items)]
        self.i += 1
        return it


D_MODEL = 2048
RMS_EPS = 1e-6


def make_ident(p, dt=BF16, name="ident"):
    ident = p.sb(name, [128, 128], dt)
    rid = Res(name)
    p.pool.op(lambda e: e.memset(ident[:], 0.0), writes=[rid])
    p.pool.op(lambda e: e.affine_select(out=ident[:], in_=ident[:], pattern=[[-1, 128]],
                                        compare_op=ALU.not_equal, fill=1.0, base=0,
                                        channel_multiplier=1), reads=[rid], writes=[rid])
    return ident, rid


def norm_transpose(p, xt, rx, hT, rhT, tok0, gT, rg, ident, rid, scr):
    junk, rjunk, ss, rss, hb, rhb, ptr = scr
    D = D_MODEL
    p.act.op(lambda e: e.activation(out=junk[:], in_=xt[:], func=AF.Square, accum_out=ss[:]),
             reads=[rx], writes=[rjunk, rss])
    p.dve.op(lambda e: e.tensor_scalar(out=ss[:], in0=ss[:], scalar1=1.0 / D, scalar2=RMS_EPS,
                                       op0=ALU.mult, op1=ALU.add), reads=[rss], writes=[rss])
    p.act.op(lambda e: e.sqrt(out=ss[:], in_=ss[:]), reads=[rss], writes=[rss])
    p.dve.op(lambda e: e.reciprocal(out=ss[:], in_=ss[:]), reads=[rss], writes=[rss])
    p.dve.op(lambda e: e.tensor_scalar(out=hb[:], in0=xt[:], scalar1=ss[:, 0:1], scalar2=None,
                                       op0=ALU.mult), reads=[rx, rss], writes=[rhb])
    for c4 in range(4):
        pt, rpt = ptr.next()
        for j in range(4):
            c = c4 * 4 + j
            p.pe.op(lambda e: e.transpose(pt[:, j * 128:(j + 1) * 128], hb[:, c * 128:(c + 1) * 128], ident[:]),
                    reads=[rhb, rid], writes=[rpt], inc=(j == 3))
        for j in range(4):
            c = c4 * 4 + j
            q = p.act if (j % 2 == 0) else p.dve
            if q is p.act:
                q.op(lambda e: e.activation(out=hT[:, c, tok0:tok0 + 128], in_=pt[:, j * 128:(j + 1) * 128],
                                            func=AF.Copy, scale=gT[:, c:c + 1]),
                     reads=[rpt, rg], writes=[rhT])
            else:
                q.op(lambda e: e.tensor_scalar(out=hT[:, c, tok0:tok0 + 128], in0=pt[:, j * 128:(j + 1) * 128],
                                               scalar1=gT[:, c:c + 1], scalar2=None, op0=ALU.mult),
                     reads=[rpt, rg], writes=[rhT])


def build_proj(N_in, fm_ranges, tm_ranges, T=2048, name="proj", host=None):
    import contextlib
    NT = T // 128
    n_fm = sum(b - a for a, b in fm_ranges)
    n_tm = sum(b - a for a, b in tm_ranges)
    if host is None:
        p = Prog(name)
        nc = p.nc
        x = nc.dram_tensor("x", [T, D_MODEL], F32, kind="ExternalInput").ap()
        g = nc.dram_tensor("gT", [128, 16], F32, kind="ExternalInput").ap()
        w = nc.dram_tensor("w", [D_MODEL, N_in], F32, kind="ExternalInput").ap()
        outT = nc.dram_tensor("outT", [n_fm, T], F32, kind="ExternalOutput").ap()
        outV = nc.dram_tensor("outV", [T, max(n_tm, 1)], F32, kind="ExternalOutput").ap()
    else:
        p, a = host
        nc = p.nc
        x, g, w, outT, outV = a["x"], a["gT"], a["w"], a["outT"], a["outV"]
    wv = w.rearrange("(c p) n -> p c n", p=128)
    out_tags = []
    with contextlib.ExitStack() as st:
        if host is None:
            p.start(st)
        else:
            p.stack = st
        ident, rid = make_ident(p)
        gT = p.sb("gTs", [128, 16], F32); rg = Res("g")
        p.sp.dma(gT[:], g[:, :], writes=[rg])
        hT = p.sb("hT", [128, 16, T], BF16); rhT = Res("hT")
        xr = Ring([(p.sb(f"x{i}", [128, D_MODEL], F32), Res(f"x{i}")) for i in range(3)])
        junk = p.sb("junk", [128, D_MODEL], BF16); rjunk = Res()
        ssr = Ring([(p.sb(f"ss{i}", [128, 1], F32), Res()) for i in range(2)])
        hbr = Ring([(p.sb(f"hb{i}", [128, D_MODEL], BF16), Res()) for i in range(2)])
        ptr = Ring([(p.ps(f"pt{i}", [128, 512], BF16), Res()) for i in range(2)])
        for t in range(NT):
            xt, rx = xr.next()
            p.sp.dma(xt[:], x[t * 128:(t + 1) * 128, :], writes=[rx])
            ss, rss = ssr.next()
            hb, rhb = hbr.next()
            norm_transpose(p, xt, rx, hT, rhT, t * 128, gT, rg, ident, rid,
                           (junk, rjunk, ss, rss, hb, rhb, ptr))
        blocks = []
        row = 0
        for a, b in fm_ranges:
            c0 = a
            while c0 < b:
                c1 = min(c0 + 512, b)
                blocks.append(("fm", c0, c1, row))
                row += c1 - c0
                c0 = c1
        col = 0
        for a, b in tm_ranges:
            c0 = a
            while c0 < b:
                c1 = min(c0 + 512, b)
                blocks.append(("tm", c0, c1, col))
                col += c1 - c0
                c0 = c1
        wr = Ring([(p.sb(f"w{i}", [128, 16, 512], BF16), Res(f"w{i}")) for i in range(2)])
        psr = Ring([(p.ps(f"py{i}", [128, 512], F32), Res()) for i in range(4)])
        stg = Ring([(p.sb(f"stg{i}", [128, T], F32), Res()) for i in range(2)])
        stv = Ring([(p.sb(f"stv{i}", [128, 512], F32), Res()) for i in range(3)])
        ev = 0
        for kind, c0, c1, o0 in blocks:
            wb, rw = wr.next()
            nb = c1 - c0
            p.pool.dma(wb[:, :, 0:nb], wv[:, :, c0:c1], writes=[rw])
            if kind == "fm":
                for j0 in range(0, nb, 128):
                    wj = min(128, nb - j0)
                    sg, rsg = stg.next()
                    for tg in range(T // 512):
                        py, rpy = psr.next()
                        for c in range(16):
                            p.pe.op(lambda e: e.matmul(py[0:wj, :], wb[:, c, j0:j0 + wj], hT[:, c, tg * 512:(tg + 1) * 512],
                                                       start=(c == 0), stop=(c == 15)),
                                    reads=[rw, rhT], writes=[rpy], inc=(c == 15))
                        ev += 1
                        if ev % 2:
                            p.act.op(lambda e: e.copy(out=sg[0:wj, tg * 512:(tg + 1) * 512], in_=py[0:wj, :]),
                                     reads=[rpy], writes=[rsg])
                        else:
                            p.dve.op(lambda e: e.tensor_copy(sg[0:wj, tg * 512:(tg + 1) * 512], py[0:wj, :]),
                                     reads=[rpy], writes=[rsg])
                    out_tags.append(p.sp.dma(outT[o0 + j0:o0 + j0 + wj, :], sg[0:wj, :], reads=[rsg]))
            else:
                for t in range(NT):
                    py, rpy = psr.next()
                    for c in range(16):
                        p.pe.op(lambda e: e.matmul(py[:, 0:nb], hT[:, c, t * 128:(t + 1) * 128], wb[:, c, 0:nb],
                                                   start=(c == 0), stop=(c == 15)),
                                reads=[rw, rhT], writes=[rpy], inc=(c == 15))
                    sv, rsv = stv.next()
                    ev += 1
                    if ev % 2:
                        p.act.op(lambda e: e.copy(out=sv[:, 0:nb], in_=py[:, 0:nb]), reads=[rpy], writes=[rsv])
                    else:
                        p.dve.op(lambda e: e.tensor_copy(sv[:, 0:nb], py[:, 0:nb]), reads=[rpy], writes=[rsv])
                    out_tags.append(p.sp.dma(outV[t * 128:(t + 1) * 128, o0:o0 + nb], sv[:, 0:nb], reads=[rsv]))
        if host is None:
            p.finish(out_tags)
        else:
            drain(p)
    return nc


def build_post(final, T=2048, TH=1024, DFF=8192, name="post", host=None):
    import contextlib
    D = D_MODEL
    if host is None:
        p = Prog(name)
        nc = p.nc
        x = nc.dram_tensor("x", [T, D], F32, kind="ExternalInput").ap()
        OT = nc.dram_tensor("OT", [D, T], F32, kind="ExternalInput").ap()
        wo = nc.dram_tensor("wo", [D, D], F32, kind="ExternalInput").ap()
        g = nc.dram_tensor("gT", [128, 16], F32, kind="ExternalInput").ap()
        wup = nc.dram_tensor("wup", [D, DFF], F32, kind="ExternalInput").ap()
        wdn = nc.dram_tensor("wdn", [DFF, D], F32, kind="ExternalInput").ap()
        if final:
            gf = nc.dram_tensor("gf", [128, D], F32, kind="ExternalInput").ap()
        xo = nc.dram_tensor("xo", [T, D], F32, kind="ExternalOutput").ap()
    else:
        p, a = host
        nc = p.nc
        x, OT, wo, g, wup, wdn, xo = a["x"], a["OT"], a["wo"], a["gT"], a["wup"], a["wdn"], a["xo"]
        if final:
            gf = a["gf"]
    OTv = OT.rearrange("(c p) t -> p c t", p=128)
    wov = wo.rearrange("(c p) n -> p c n", p=128)
    wupv = wup.rearrange("(c p) n -> p c n", p=128)
    wdnv = wdn.rearrange("(j p) n -> p j n", p=128)
    NTH = TH // 128
    FB = 256
    out_tags = []
    with contextlib.ExitStack() as st:
        if host is None:
            p.start(st)
        else:
            p.stack = st
        ident, rid = make_ident(p)
        gT = p.sb("gTs", [128, 16], F32); rg = Res("g")
        p.sp.dma(gT[:], g[:, :], writes=[rg])
        if final:
            gfs = p.sb("gfs", [128, D], F32); rgf = Res("gf")
            p.sp.dma(gfs[:], gf[:, :], writes=[rgf])
        actT = p.sb("actT", [128, 16, TH], BF16); ract = Res("actT")
        xs = [(p.sb(f"xs{i}", [128, D], F32), Res(f"xs{i}")) for i in range(NTH)]
        junk = p.sb("junk", [128, D], BF16); rjunk = Res()
        ssr = Ring([(p.sb(f"ss{i}", [128, 1], F32), Res()) for i in range(2)])
        hbr = Ring([(p.sb(f"hb{i}", [128, D], BF16), Res()) for i in range(2)])
        ptr = Ring([(p.ps(f"pt{i}", [128, 512], BF16), Res()) for i in range(2)])
        wur = Ring([(p.sb(f"wu{i}", [128, 16, FB], BF16), Res(f"wu{i}")) for i in range(2)])
        wdr = Ring([(p.sb(f"wd{i}", [128, FB // 128, D], BF16), Res(f"wd{i}")) for i in range(2)])
        uTr = Ring([(p.sb(f"uT{i}", [128, FB // 128, TH], BF16), Res(f"uT{i}")) for i in range(2)])
        rsr = Ring([(p.sb(f"rs{i}", [128, 512], F32), Res()) for i in range(2)])
        pyr = Ring([(p.ps(f"py{i}", [128, 512], F32), Res()) for i in range(3)])
        pur = Ring([(p.ps(f"pu{i}", [128, 512], F32), Res()) for i in range(2)])
        for hh in range(T // TH):
            t0 = hh * TH
            p.pool.dma(actT[:, :, :], OTv[:, :, t0:t0 + TH], writes=[ract])
            for t in range(NTH):
                xt, rx = xs[t]
                p.sp.dma(xt[:], x[t0 + t * 128:t0 + (t + 1) * 128, :], writes=[rx])
            for nb in range(D // FB):
                wb, rw = wur.next()
                p.pool.dma(wb[:, :, :], wov[:, :, nb * FB:(nb + 1) * FB], writes=[rw])
                for t in range(NTH):
                    xt, rx = xs[t]
                    py, rpy = pyr.next()
                    for c in range(16):
                        p.pe.op(lambda e: e.matmul(py[:, 0:FB], actT[:, c, t * 128:(t + 1) * 128], wb[:, c, :],
                                                   start=(c == 0), stop=(c == 15)),
                                reads=[ract, rw], writes=[rpy], inc=(c == 15))
                    p.dve.op(lambda e: e.tensor_tensor(out=xt[:, nb * FB:(nb + 1) * FB], in0=xt[:, nb * FB:(nb + 1) * FB],
                                                       in1=py[:, 0:FB], op=ALU.add),
                             reads=[rpy, rx], writes=[rx])
            for t in range(NTH):
                xt, rx = xs[t]
                ss, rss = ssr.next()
                hb, rhb = hbr.next()
                norm_transpose(p, xt, rx, actT, ract, t * 128, gT, rg, ident, rid,
                               (junk, rjunk, ss, rss, hb, rhb, ptr))
            for fb in range(DFF // FB):
                wu, rwu = wur.next()
                p.pool.dma(wu[:, :, :], wupv[:, :, fb * FB:(fb + 1) * FB], writes=[rwu])
                wd, rwd = wdr.next()
                p.pool.dma(wd[:, :, :], wdnv[:, fb * (FB // 128):(fb + 1) * (FB // 128), :], writes=[rwd])
                uT, ruT = uTr.next()
                for j in range(FB // 128):
                    for tg in range(TH // 512):
                        pu, rpu = pur.next()
                        for c in range(16):
                            p.pe.op(lambda e: e.matmul(pu[:, :], wu[:, c, j * 128:(j + 1) * 128],
                                                       actT[:, c, tg * 512:(tg + 1) * 512],
                                                       start=(c == 0), stop=(c == 15)),
                                    reads=[ract, rwu], writes=[rpu], inc=(c == 15))
                        rs, rrs = rsr.next()
                        p.act.op(lambda e: e.activation(out=rs[:], in_=pu[:], func=AF.Relu),
                                 reads=[rpu], writes=[rrs])
                        p.pool.op(lambda e: e.tensor_tensor(out=uT[:, j, tg * 512:(tg + 1) * 512], in0=rs[:], in1=rs[:],
                                                            op=ALU.mult), reads=[rrs], writes=[ruT])
                for t in range(NTH):
                    xt, rx = xs[t]
                    for nb in range(D // 512):
                        py, rpy = pyr.next()
                        nj = FB // 128
                        for j in range(nj):
                            p.pe.op(lambda e: e.matmul(py[:, :], uT[:, j, t * 128:(t + 1) * 128],
                                                       wd[:, j, nb * 512:(nb + 1) * 512],
                                                       start=(j == 0), stop=(j == nj - 1)),
                                    reads=[ruT, rwd], writes=[rpy], inc=(j == nj - 1))
                        p.dve.op(lambda e: e.tensor_tensor(out=xt[:, nb * 512:(nb + 1) * 512],
                                                           in0=xt[:, nb * 512:(nb + 1) * 512],
                                                           in1=py[:, :], op=ALU.add),
                                 reads=[rpy, rx], writes=[rx])
            for t in range(NTH):
                xt, rx = xs[t]
                if final:
                    ss, rss = ssr.next()
                    p.act.op(lambda e: e.activation(out=junk[:], in_=xt[:], func=AF.Square, accum_out=ss[:]),
                             reads=[rx], writes=[rjunk, rss])
                    p.dve.op(lambda e: e.tensor_scalar(out=ss[:], in0=ss[:], scalar1=1.0 / D, scalar2=RMS_EPS,
                                                       op0=ALU.mult, op1=ALU.add), reads=[rss], writes=[rss])
                    p.act.op(lambda e: e.sqrt(out=ss[:], in_=ss[:]), reads=[rss], writes=[rss])
                    p.dve.op(lambda e: e.reciprocal(out=ss[:], in_=ss[:]), reads=[rss], writes=[rss])
                    p.dve.op(lambda e: e.tensor_scalar(out=xt[:], in0=xt[:], scalar1=ss[:, 0:1], scalar2=None,
                                                       op0=ALU.mult), reads=[rx, rss], writes=[rx])
                    p.dve.op(lambda e: e.tensor_tensor(out=xt[:], in0=xt[:], in1=gfs[:], op=ALU.mult),
                             reads=[rx, rgf], writes=[rx])
                out_tags.append(p.sp.dma(xo[t0 + t * 128:t0 + (t + 1) * 128, :], xt[:], reads=[rx]))
        if host is None:
            p.finish(out_tags)
        else:
            if final:
                host[1]["out_tags"] += out_tags
            drain(p)
    return nc


class AttnCtx:
    def __init__(self, p, n_s=3, n_pt=4):
        self.p = p
        self.sr = Ring([(p.ps(f"psS{i}", [128, 512], F32), Res(f"psS{i}")) for i in range(n_s)])
        self.ptr = Ring([(p.sb(f"pT{i}", [128, 512], BF16), Res(f"pT{i}")) for i in range(n_pt)])


def run_jobs(ctx, jobs, lookahead=2, late=2):
    p = ctx.p
    n = len(jobs)
    staged = {}
    pend_late = []

    def emit_s(i):
        jb = jobs[i]
        if "pre" in jb:
            jb["pre"]()
        if "pre2" in jb:
            jb["pre2"]()
        ps, rps = ctx.sr.next()
        kp = jb["kp"]
        mm = jb["s_mms"]
        for k, (lhsT, rhs, reads) in enumerate(mm):
            p.pe.op(lambda e: e.matmul(ps[0:kp, :], lhsT, rhs, start=(k == 0), stop=(k == len(mm) - 1)),
                    reads=reads, writes=[rps], inc=(k == len(mm) - 1))
        pt, rpt = ctx.ptr.next()
        b = jb.get("bias")
        if b is None:
            p.act.op(lambda e: e.activation(out=pt[0:kp, :], in_=ps[0:kp, :], func=AF.Exp, scale=jb["scale"]),
                     reads=[rps], writes=[rpt])
        else:
            p.act.op(lambda e: e.activation(out=pt[0:kp, :], in_=ps[0:kp, :], func=AF.Exp, scale=jb["scale"],
                                            bias=b[0]), reads=[rps] + list(b[1]), writes=[rpt])
        staged[i] = (pt, rpt)

    def emit_pv(i):
        jb = jobs[i]
        pt, rpt = staged.pop(i)
        kp = jb["kp"]
        for (out_ap, lhsT, rout, reads, start, stop) in jb["pv"]:
            p.pe.op(lambda e: e.matmul(out_ap, lhsT, pt[0:kp, :], start=start, stop=stop),
                    reads=[rpt] + list(reads), writes=[rout], inc=stop)
        d = jb.get("done")
        if d is not None:
            lt = d()
            if lt is not None:
                pend_late.append([late, lt])

    def tick_late(force=False):
        for it in list(pend_late):
            it[0] -= 1
            if it[0] <= 0 or force:
                it[1]()
                pend_late.remove(it)

    for i in range(min(lookahead, n)):
        emit_s(i)
    for i in range(n):
        if i + lookahead < n:
            emit_s(i + lookahead)
        emit_pv(i)
        tick_late()
    while pend_late:
        tick_late(force=True)


NEG = -240000.0


def causal_masks():
    k = np.arange(128)[:, None, None]
    j = np.arange(4)[None, :, None]
    q = np.arange(512)[None, None, :]
    return np.where(128 * j + k <= q, 0.0, NEG).astype(np.float32)


def make_normalizer(p, n_bc=1):
    ones = p.sb("ones65", [65, 64], F32)
    rones = Res("ones65")
    p.pool.op(lambda e: e.memset(ones[:], 1.0), writes=[rones])
    bcr = Ring([(p.ps(f"pbc{i}", [64, 512], F32), Res(f"pbc{i}")) for i in range(n_bc)])
    rdr = Ring([(p.sb(f"rden{i}", [65, 512], F32), Res(f"rden{i}")) for i in range(3)])
    osr = Ring([(p.sb(f"osb{i}", [64, 512], F32), Res(f"osb{i}")) for i in range(3)])
    return dict(ones=ones, rones=rones, bcr=bcr, rdr=rdr, osr=osr)


def build_fox(NH, T=4096, name="fox", host=None):
    import contextlib
    if host is None:
        p = Prog(name)
        nc = p.nc
        qT = nc.dram_tensor("qT", [NH * 64, T], F32, kind="ExternalInput").ap()
        kT = nc.dram_tensor("kT", [NH * 64, T], F32, kind="ExternalInput").ap()
        v = nc.dram_tensor("v", [T, NH * 64], F32, kind="ExternalInput").ap()
        fT = nc.dram_tensor("fT", [NH, T], F32, kind="ExternalInput").ap()
        fb = nc.dram_tensor("fb", [NH, 1], F32, kind="ExternalInput").ap()
        mk = nc.dram_tensor("masks", [128, 4, 512], F32, kind="ExternalInput").ap()
        OT = nc.dram_tensor("OT", [NH * 64, T], F32, kind="ExternalOutput").ap()
    else:
        p, a = host
        nc = p.nc
        qT, kT, v, fT, fb, mk, OT = a["qT"], a["kT"], a["v"], a["fT"], a["fb"], a["masks"], a["OT"]
    NB = T // 128
    NQC = T // 512
    vv = v.rearrange("(n p) c -> p n c", p=128)
    out_tags = []
    with contextlib.ExitStack() as st:
        if host is None:
            p.start(st)
        else:
            p.stack = st
        ident, rid = make_ident(p)
        nz = make_normalizer(p)
        masks = p.sb("masks_sb", [128, 4, 512], BF16); rmask = Res("masks")
        p.pool.dma(masks[:], mk[:, :, :], writes=[rmask])
        fl = p.sb("fl", [NH, T], F32); rfl = Res("fl")
        p.sp.dma(fl[:], fT[:, :], writes=[rfl])
        fbs = p.sb("fbs", [NH, 1], F32); rfb = Res("fb")
        p.sp.dma(fbs[:], fb[:, :], writes=[rfb])
        p.dve.op(lambda e: e.tensor_scalar(out=fbs[:], in0=fbs[:], scalar1=-1.0, scalar2=None, op0=ALU.mult),
                 reads=[rfb], writes=[rfb])
        p.act.op(lambda e: e.activation(out=fl[:], in_=fl[:], func=AF.Exp, scale=-1.0, bias=fbs[:, 0:1]),
                 reads=[rfl, rfb], writes=[rfl])
        p.act.op(lambda e: e.activation(out=fl[:], in_=fl[:], func=AF.Ln, bias=1.0, scale=1.0),
                 reads=[rfl], writes=[rfl])
        zer = p.sb("zer", [NH, T], F32); rz = Res("zer")
        p.pool.op(lambda e: e.memset(zer[:], 0.0), writes=[rz])
        p.dve.op(lambda e: e.tensor_scalar(out=fl[:], in0=fl[:], scalar1=-8.0, scalar2=None, op0=ALU.mult),
                 reads=[rfl], writes=[rfl])
        c8 = p.sb("c8", [NH, T], F32); rc8 = Res("c8")
        p.dve.op(lambda e: e.tensor_tensor_scan(out=c8[:], data0=fl[:], data1=zer[:], initial=0.0,
                                                op0=ALU.add, op1=ALU.add), reads=[rfl, rz], writes=[rc8])
        cq = p.sb("cq", [NH, 3, T], BF16); rcq = Res("cq")
        ck = p.sb("ck", [NH, 3, T], BF16); rck = Res("ck")
        for i in range(3):
            p.dve.op(lambda e: e.tensor_copy(cq[:, i, :], c8[:]), reads=[rc8], writes=[rcq])
            if i < 2:
                p.dve.op(lambda e: e.tensor_tensor(out=c8[:], in0=c8[:], in1=cq[:, i, :], op=ALU.subtract),
                         reads=[rc8, rcq], writes=[rc8])
            p.dve.op(lambda e: e.tensor_scalar(out=ck[:, i, :], in0=cq[:, i, :], scalar1=-1.0, scalar2=None,
                                               op0=ALU.mult), reads=[rcq], writes=[rck])
        Qr = Ring([(p.sb(f"Qa{i}", [128, T], BF16), Res(f"Qa{i}")) for i in range(2)])
        Kr = Ring([(p.sb(f"Ka{i}", [128, T], BF16), Res(f"Ka{i}")) for i in range(2)])
        Vr = Ring([(p.sb(f"Va{i}", [128, NB, 65], BF16), Res(f"Va{i}")) for i in range(2)])
        for (t_, r_) in Qr.items + Kr.items:
            p.pool.op(lambda e: e.memset(t_[:, :], 0.0), writes=[r_])
            p.pool.op(lambda e: e.memset(t_[64:70, :], 1.0), writes=[r_])
        for (t_, r_) in Vr.items:
            p.pool.op(lambda e: e.memset(t_[:, :, 64:65], 1.0), writes=[r_])
        por = Ring([(p.ps(f"po{i}", [65, 512], F32), Res(f"po{i}")) for i in range(2)])
        ohr = Ring([(p.sb(f"oh{i}", [64, T], F32), Res(f"oh{i}")) for i in range(2)])
        ctx = AttnCtx(p)
        jobs = []
        for h in range(NH):
            Qa, rQ = Qr.next(); Ka, rK = Kr.next(); Va, rV = Vr.next(); oh, roh = ohr.next()

            def load(h=h, Qa=Qa, rQ=rQ, Ka=Ka, rK=rK, Va=Va, rV=rV):
                p.pool.dma(Qa[0:64, :], qT[h * 64:(h + 1) * 64, :], writes=[rQ])
                p.pool.dma(Ka[0:64, :], kT[h * 64:(h + 1) * 64, :], writes=[rK])
                p.pool.dma(Va[:, :, 0:64], vv[:, :, h * 64:(h + 1) * 64], writes=[rV])
                for i in range(3):
                    p.sp.dma(Qa[64 + i:65 + i, :], cq[h:h + 1, i, :], reads=[rcq], writes=[rQ])
                    p.sp.dma(Ka[67 + i:68 + i, :], ck[h:h + 1, i, :], reads=[rck], writes=[rK])
            first = True
            for qc in range(NQC):
                po, rpo = por.next()
                nkb = 4 * qc + 4
                for kb in range(nkb):
                    mm = [(Ka[:, kb * 128:(kb + 1) * 128], Qa[:, qc * 512:(qc + 1) * 512], [rQ, rK])]
                    if kb >= 4 * qc:
                        mm.append((ident[:], masks[:, kb - 4 * qc, :], [rid, rmask]))
                    jb = dict(s_mms=mm, kp=128, scale=0.125,
                              pv=[(po[0:65, :], Va[:, kb, 0:65], rpo, [rV], kb == 0, kb == nkb - 1)])
                    if first:
                        jb["pre"] = load
                        first = False
                    if kb == nkb - 1:
                        jb["done"] = make_norm_done(p, nz, po, rpo, oh, roh, qc,
                                                    final=(lambda h=h, oh=oh, roh=roh: out_tags.append(
                                                        p.sp.dma(OT[h * 64:(h + 1) * 64, :], oh[:], reads=[roh])))
                                                    if qc == NQC - 1 else None)
                    jobs.append(jb)
        run_jobs(ctx, jobs)
        if host is None:
            p.finish(out_tags)
        else:
            drain(p)
    return nc


def make_norm_done(p, nz, po, rpo, oh, roh, qc, final=None, sink=None):
    def done():
        rd, rrd = nz["rdr"].next()
        osb, ros = nz["osr"].next()
        if sink is not None:
            es, res_ = sink
            p.dve.op(lambda e: e.tensor_scalar(out=rd[64:65, :], in0=po[64:65, :], scalar1=es, scalar2=None,
                                               op0=ALU.add), reads=[rpo, res_], writes=[rrd])
            p.dve.op(lambda e: e.reciprocal(out=rd[64:65, :], in_=rd[64:65, :]), reads=[rrd], writes=[rrd])
        else:
            p.dve.op(lambda e: e.reciprocal(out=rd[64:65, :], in_=po[64:65, :]), reads=[rpo], writes=[rrd])
        p.act.op(lambda e: e.copy(out=osb[:], in_=po[0:64, :]), reads=[rpo], writes=[ros])

        def late():
            bc, rbc = nz["bcr"].next()
            p.pe.op(lambda e: e.matmul(bc[:, :], nz["ones"][64:65, 0:64], rd[64:65, :], start=True, stop=True),
                    reads=[nz["rones"], rrd], writes=[rbc])
            p.dve.op(lambda e: e.tensor_tensor(out=oh[:, qc * 512:(qc + 1) * 512], in0=osb[:], in1=bc[:, :],
                                               op=ALU.mult), reads=[ros, rbc], writes=[roh])
            if final is not None:
                final()
        return late
    return done


HD = 64
NSA_W1 = 2048
MASKV = -30000.0


def rel_bucket_np(dist):
    import math
    n = np.maximum(dist, 0)
    nf = np.maximum(n, 1).astype(np.float32)
    large = 16 + (np.log(nf / np.float32(16)) / np.float32(math.log(1024 / 16)) * np.float32(16)).astype(np.int32)
    return np.where(n < 16, n, np.minimum(large, 31))


def skew_bias(tabcol, r_max, width, lo, hi):
    ki = np.arange(128)[:, None]
    X = np.arange(width)[None, :]
    dist = X - ki - 128 * r_max
    val = tabcol[rel_bucket_np(dist)]
    return np.where((dist >= lo) & (dist < hi), val, np.float32(MASKV)).astype(np.float32)


def cmp_bias(tabcol):
    ci = np.arange(128)[:, None]
    X = np.arange(4096)[None, :]
    dc = X - 16 * ci - 31
    val = tabcol[rel_bucket_np(dc)]
    return np.where(dc >= 0, val, np.float32(MASKV)).astype(np.float32)


def nsa_consts():
    c = np.arange(256)
    s = np.arange(64)
    cstart = c * 16
    ov = ((cstart[:, None] < s[None, :] * 64 + 64) & (cstart[:, None] + 32 > s[None, :] * 64)).astype(np.float32)
    ov[255, :] = 0.0
    ovl = np.ascontiguousarray(ov.reshape(2, 128, 64).transpose(1, 0, 2))
    esel = (np.arange(4096)[None, :] // 64 == s[:, None]).astype(np.float32)
    q = np.arange(4096)
    cur = q // 64
    forced = (s[None, :] == 0) | (s[None, :] == cur[:, None]) | (s[None, :] == cur[:, None] - 1)
    future = (s[None, :] * 64) > q[:, None]
    A = np.where(future | forced, 0.0, 1.0).astype(np.float32)
    Bm = np.where(future, -1e30, np.where(forced, 1e4, 0.0)).astype(np.float32)
    tkA = np.ascontiguousarray(A.reshape(32, 128, 64).transpose(1, 0, 2))
    tkB = np.ascontiguousarray(Bm.reshape(32, 128, 64).transpose(1, 0, 2))
    selg = np.zeros((24, 24, 65), np.float32)
    for r in range(24):
        selg[r, r, :] = 1.0
    return dict(ovl=ovl, esel=esel, tkA=tkA, tkB=tkB, selg=selg.reshape(24, 24 * 65))


def build_even(T=4096, do_swa=True, do_nsa=True, name="even", NH=8, host=None):
    import contextlib
    NB = T // 128
    NQC = T // 512
    if host is None:
        p = Prog(name)
        nc = p.nc

        def din(nm, shape):
            return nc.dram_tensor(nm, list(shape), F32, kind="ExternalInput").ap()
        OT = nc.dram_tensor("OT", [2 * NH * 64, T], F32, kind="ExternalOutput").ap()
        OT_nsa = OT[NH * 64:2 * NH * 64, :]
    else:
        p, a = host
        nc = p.nc

        def din(nm, shape):
            ap = a[nm]
            assert list(ap.shape) == list(shape), (nm, ap.shape, shape)
            return ap
        OT = a["OT_swa"]
        OT_nsa = a["OT_nsa"]
    out_tags = []
    with contextlib.ExitStack() as st:
        if host is None:
            p.start(st)
        else:
            p.stack = st
        ident, rid = make_ident(p)
        ident8 = p.sb("ident8", [128, 128], BF16); rid8 = Res("ident8")
        p.dve.op(lambda e: e.tensor_scalar(out=ident8[:], in0=ident[:], scalar1=8.0, scalar2=None, op0=ALU.mult),
                 reads=[rid], writes=[rid8])
        nz = make_normalizer(p)
        ctx = AttnCtx(p)
        por = Ring([(p.ps(f"po{i}", [65, 512], F32), Res(f"po{i}")) for i in range(2)])
        if do_swa:
            sqT = din("sqT", [NH * 64, T]); skT = din("skT", [128, T]); sv = din("sv", [T, 128])
            sinks = din("sinks", [1, NH]); Wswa = din("Wswa", [NH, 128, 1024])
            with contextlib.ExitStack() as st2:
                p.stack = st2
                KT = p.sb("sKT", [128, 2, T], BF16); rKT = Res("sKT")
                p.pool.op(lambda e: e.memset(KT[64:128, :, :], 0.0), writes=[rKT])
                p.pool.dma(KT[0:64], skT.rearrange("(g d) t -> d g t", d=64), writes=[rKT])
                Va = p.sb("sVa", [128, 2, NB, 65], BF16); rVa = Res("sVa")
                p.pool.op(lambda e: e.memset(Va[:, :, :, 64:65], 1.0), writes=[rVa])
                for g in range(2):
                    p.pool.dma(Va[:, g, :, 0:64], sv.rearrange("(n p) c -> p n c", p=128)[:, :, g * 64:(g + 1) * 64],
                               writes=[rVa])
                es = p.sb("esink", [65, NH], F32); res_ = Res("esink")
                p.sp.dma(es[64:65, :], sinks[:, :], writes=[res_])
                p.act.op(lambda e: e.activation(out=es[64:65, :], in_=es[64:65, :], func=AF.Exp),
                         reads=[res_], writes=[res_])
                Qr = Ring([(p.sb(f"sQ{i}", [128, T], BF16), Res(f"sQ{i}")) for i in range(2)])
                for (t_, r_) in Qr.items:
                    p.pool.op(lambda e: e.memset(t_[64:128, :], 0.0), writes=[r_])
                Wr = Ring([(p.sb(f"sW{i}", [128, 1024], BF16), Res(f"sW{i}")) for i in range(2)])
                ohr = Ring([(p.sb(f"soh{i}", [64, T], F32), Res(f"soh{i}")) for i in range(2)])
                jobs = []
                for h in range(NH):
                    g = h // 4
                    Q, rQ = Qr.next(); W, rW = Wr.next(); oh, roh = ohr.next()

                    def load(h=h, Q=Q, rQ=rQ, W=W, rW=rW):
                        p.pool.dma(Q[0:64, :], sqT[h * 64:(h + 1) * 64, :], writes=[rQ])
                        p.pool.dma(W[:, :], Wswa[h, :, :], writes=[rW])
                    first = True
                    for qc in range(NQC):
                        po, rpo = por.next()
                        kbs = list(range(max(0, 4 * qc - 1), 4 * qc + 4))
                        for kb in kbs:
                            r = kb - 4 * qc
                            mm = [(KT[:, g, kb * 128:(kb + 1) * 128], Q[:, qc * 512:(qc + 1) * 512], [rKT, rQ]),
                                  (ident8[:], W[:, 128 * (3 - r):128 * (3 - r) + 512], [rid8, rW])]
                            jb = dict(s_mms=mm, kp=128, scale=0.125,
                                      pv=[(po[0:65, :], Va[:, g, kb, 0:65], rpo, [rVa], kb == kbs[0], kb == kbs[-1])])
                            if first:
                                jb["pre"] = load
                                first = False
                            if kb == kbs[-1]:
                                fin = None
                                if qc == NQC - 1:
                                    fin = (lambda h=h, oh=oh, roh=roh: out_tags.append(
                                        p.sp.dma(OT[h * 64:(h + 1) * 64, :], oh[:], reads=[roh])))
                                jb["done"] = make_norm_done(p, nz, po, rpo, oh, roh, qc, final=fin,
                                                            sink=(es[64:65, h:h + 1], res_))
                            jobs.append(jb)
                run_jobs(ctx, jobs)
                drain(p)
            p.stack = st
        if do_nsa:
            build_nsa_part(p, st, din, OT_nsa, out_tags, ident, rid, ident8, rid8, nz, ctx, por, T, NH)
        if host is None:
            p.finish(out_tags)
        else:
            drain(p)
    return nc


def drain(p):
    qs = [p.pe, p.act, p.dve, p.pool, p.sp]
    for q in qs:
        for o in qs:
            if o is not q and o.ctr.n > 0:
                q._wait(o.ctr, o.ctr.n)
        for o in qs:
            for sl in o.dma_slots:
                if sl.n > 0:
                    q._wait(sl, sl.n)


def make_gated_done(p, nz, S, po, rpo, h, br, gs, rgs, first, acc_ap, racc, cmp_extra=None, final=None):
    NG = S["NG"]

    def done():
        rd, rrd = nz["rdr"].next()
        osb, ros = nz["osr"].next()
        p.dve.op(lambda e: e.tensor_scalar(out=rd[64:65, :], in0=po[64:65, :], scalar1=1e-30, scalar2=None,
                                           op0=ALU.max), reads=[rpo], writes=[rrd])
        p.dve.op(lambda e: e.reciprocal(out=rd[64:65, :], in_=rd[64:65, :]), reads=[rrd], writes=[rrd])
        p.act.op(lambda e: e.copy(out=osb[:], in_=po[0:64, :]), reads=[rpo], writes=[ros])
        if cmp_extra is not None:
            pu, rpu, imp, rimp = cmp_extra
            pus, rpus = S["pusr"].next()
            p.act.op(lambda e: e.copy(out=pus[:], in_=pu[0:64, :]), reads=[rpu], writes=[rpus])

        def late():
            gbc, rgbc = S["gbc"]
            r = 3 * h + br
            p.pe.op(lambda e: e.matmul(gbc[0:65, :], S["selg"][0:NG, r * 65:(r + 1) * 65], gs[0:NG, :],
                                       start=True, stop=True), reads=[S["rselg"], rgs], writes=[rgbc])
            rd2, rrd2 = S["rd2r"].next()
            p.dve.op(lambda e: e.tensor_tensor(out=rd2[64:65, :], in0=rd[64:65, :], in1=gbc[64:65, :], op=ALU.mult),
                     reads=[rrd, rgbc], writes=[rrd2])
            bc, rbc = nz["bcr"].next()
            p.pe.op(lambda e: e.matmul(bc[:, :], nz["ones"][64:65, 0:64], rd2[64:65, :], start=True, stop=True),
                    reads=[nz["rones"], rrd2], writes=[rbc])
            if first:
                p.dve.op(lambda e: e.tensor_tensor(out=acc_ap, in0=osb[:], in1=bc[:, :], op=ALU.mult),
                         reads=[ros, rbc], writes=[racc])
            else:
                tmp, rtmp = S["tmpr"].next()
                p.dve.op(lambda e: e.tensor_tensor(out=tmp[:], in0=osb[:], in1=bc[:, :], op=ALU.mult),
                         reads=[ros, rbc], writes=[rtmp])
                p.dve.op(lambda e: e.tensor_tensor(out=acc_ap, in0=acc_ap, in1=tmp[:], op=ALU.add),
                         reads=[rtmp, racc], writes=[racc])
            if cmp_extra is not None:
                p.pe.op(lambda e: e.matmul(gbc[0:64, :], nz["ones"][64:65, 0:64], rd[64:65, :], start=True, stop=True),
                        reads=[nz["rones"], rrd], writes=[rgbc])
                if h == 0:
                    p.dve.op(lambda e: e.tensor_tensor(out=imp[:], in0=pus[:], in1=gbc[0:64, :], op=ALU.mult),
                             reads=[rpus, rgbc], writes=[rimp])
                else:
                    tmp2, rtmp2 = S["tmpr"].next()
                    p.dve.op(lambda e: e.tensor_tensor(out=tmp2[:], in0=pus[:], in1=gbc[0:64, :], op=ALU.mult),
                             reads=[rpus, rgbc], writes=[rtmp2])
                    p.dve.op(lambda e: e.tensor_tensor(out=imp[:], in0=imp[:], in1=tmp2[:], op=ALU.add),
                             reads=[rtmp2, rimp], writes=[rimp])
            if final is not None:
                final()
        return late
    return done


def build_nsa_part(p, st, din, OT, out_tags, ident, rid, ident8, rid8, nz, ctx, por, T, NH):
    import contextlib
    NB = T // 128
    NQC = T // 512
    NG = 3 * NH
    nqT = din("nqT", [NH * 64, T]); kcT = din("kcT", [64, T]); vcT = din("vcT", [64, T])
    ksT = din("ksT", [64, T]); vs = din("vs", [T, 64]); kwT = din("kwT", [64, T]); vw = din("vw", [T, 64])
    gTd = din("gT", [NG, T]); pekT = din("pekT", [64, 32]); pevT = din("pevT", [64, 32])
    w1k = din("w1k", [2048, 256]); w2k = din("w2k", [256, 64]); w1v = din("w1v", [2048, 256]); w2v = din("w2v", [256, 64])
    Wc = din("Wc", [NH, 128, 4096]); Wsel = din("Wsel", [NH, 128, 2048]); Wwin = din("Wwin", [NH, 128, 1408])
    ovl_d = din("ovl", [128, 2, 64]); esel_d = din("esel", [64, 4096])
    tkA_d = din("tkA", [128, 32, 64]); tkB_d = din("tkB", [128, 32, 64]); selg_d = din("selg", [24, 24 * 65])
    nqv = nqT.rearrange("(h d) t -> d h t", d=64)
    OTn = OT.rearrange("(h d) t -> d h t", d=64)

    kcmpT = p.sb("kcmpT", [128, 256], BF16); rkc = Res("kcmpT")
    vcmpa = p.sb("vcmpa", [128, 2, 65], BF16); rvc = Res("vcmpa")
    p.pool.op(lambda e: e.memset(kcmpT[:], 0.0), writes=[rkc])
    p.pool.op(lambda e: e.memset(vcmpa[:, :, 64:65], 1.0), writes=[rvc])
    with contextlib.ExitStack() as st3:
        p.stack = st3
        for which, xd, w1d, w2d, ped in (("k", kcT, w1k, w2k, pekT), ("v", vcT, w1v, w2v, pevT)):
            xT = p.sb(f"cx{which}", [64, T], BF16); rx = Res()
            p.pool.dma(xT[:], xd[:, :], writes=[rx])
            w1 = p.sb(f"cw1{which}", [64, 32, 256], BF16); rw1 = Res()
            p.pool.dma(w1[:], w1d.rearrange("(l d) f -> d l f", d=64), writes=[rw1])
            w2 = p.sb(f"cw2{which}", [128, 2, 64], BF16); rw2 = Res()
            p.pool.dma(w2[:], w2d.rearrange("(c p) d -> p c d", p=128), writes=[rw2])
            peT = p.sb(f"cpe{which}", [64, 32], BF16); rpe = Res()
            p.pool.dma(peT[:], ped[:, :], writes=[rpe])
            hid = p.sb(f"chid{which}", [128, 2, 256], BF16); rhid = Res()
            p.pool.op(lambda e: e.memset(hid[:], 0.0), writes=[rhid])
            bsb = p.sb(f"cb{which}", [128, 1], F32); rbsb = Res()
            xx = p.sb(f"cxx{which}", [128, 256], F32); rxx = Res()
            tt = p.sb(f"ctt{which}", [128, 256], F32); rtt = Res()
            for fc in range(2):
                ph, rph = ctx.sr.next()
                for l in range(32):
                    p.pe.op(lambda e: e.matmul(ph[:, 0:255], w1[:, l, fc * 128:(fc + 1) * 128],
                                               xT[:, l:l + 16 * 254 + 1:16], start=(l == 0), stop=(l == 31)),
                            reads=[rw1, rx], writes=[rph], inc=(l == 31))
                pb, rpb = ctx.sr.next()
                for l in range(32):
                    p.pe.op(lambda e: e.matmul(pb[:, 0:1], w1[:, l, fc * 128:(fc + 1) * 128], peT[:, l:l + 1],
                                               start=(l == 0), stop=(l == 31)),
                            reads=[rw1, rpe], writes=[rpb], inc=(l == 31))
                p.act.op(lambda e: e.copy(out=bsb[:], in_=pb[:, 0:1]), reads=[rpb], writes=[rbsb])
                p.dve.op(lambda e: e.tensor_scalar(out=xx[:, 0:255], in0=ph[:, 0:255], scalar1=bsb[:, 0:1], scalar2=None,
                                                   op0=ALU.add), reads=[rph, rbsb], writes=[rxx])
                p.dve.op(lambda e: e.tensor_tensor(out=tt[:, 0:255], in0=xx[:, 0:255], in1=xx[:, 0:255], op=ALU.mult),
                         reads=[rxx], writes=[rtt])
                p.dve.op(lambda e: e.tensor_scalar(out=tt[:, 0:255], in0=tt[:, 0:255], scalar1=0.044715, scalar2=1.0,
                                                   op0=ALU.mult, op1=ALU.add), reads=[rtt], writes=[rtt])
                p.dve.op(lambda e: e.tensor_tensor(out=tt[:, 0:255], in0=tt[:, 0:255], in1=xx[:, 0:255], op=ALU.mult),
                         reads=[rtt, rxx], writes=[rtt])
                p.act.op(lambda e: e.activation(out=tt[:, 0:255], in_=tt[:, 0:255], func=AF.Sigmoid,
                                                scale=1.5957691216057308), reads=[rtt], writes=[rtt])
                p.dve.op(lambda e: e.tensor_tensor(out=hid[:, fc, 0:255], in0=xx[:, 0:255], in1=tt[:, 0:255], op=ALU.mult),
                         reads=[rtt, rxx], writes=[rhid])
            if which == "k":
                pk, rpk = ctx.sr.next()
                for fc in range(2):
                    p.pe.op(lambda e: e.matmul(pk[0:64, 0:255], w2[:, fc, :], hid[:, fc, 0:255],
                                               start=(fc == 0), stop=(fc == 1)),
                            reads=[rw2, rhid], writes=[rpk], inc=(fc == 1))
                p.act.op(lambda e: e.copy(out=kcmpT[0:64, 0:255], in_=pk[0:64, 0:255]), reads=[rpk], writes=[rkc])
            else:
                for cc in range(2):
                    pv, rpv = ctx.sr.next()
                    for fc in range(2):
                        p.pe.op(lambda e: e.matmul(pv[:, 0:64], hid[:, fc, cc * 128:(cc + 1) * 128], w2[:, fc, :],
                                                   start=(fc == 0), stop=(fc == 1)),
                                reads=[rw2, rhid], writes=[rpv], inc=(fc == 1))
                    p.act.op(lambda e: e.copy(out=vcmpa[:, cc, 0:64], in_=pv[:, 0:64]), reads=[rpv], writes=[rvc])
        drain(p)
    p.stack = st
    ovl = p.sb("ovl_sb", [128, 2, 64], BF16); rovl = Res("ovl")
    p.pool.dma(ovl[:], ovl_d[:, :, :], writes=[rovl])
    esel = p.sb("esel_sb", [128, 4096], BF16); resel = Res("esel")
    p.pool.op(lambda e: e.memset(esel[64:128, :], 0.0), writes=[resel])
    p.pool.dma(esel[0:64, :], esel_d[:, :], writes=[resel])
    selg = p.sb("selg_sb", [24, 24 * 65], F32); rselg = Res("selg")
    p.sp.dma(selg[:], selg_d[:, :], writes=[rselg])
    identf, ridf = make_ident(p, F32, "identf")
    KsT = p.sb("KsT", [128, T], BF16); rKs = Res("KsT")
    p.pool.op(lambda e: e.memset(KsT[64:128, :], 0.0), writes=[rKs])
    p.pool.dma(KsT[0:64, :], ksT[:, :], writes=[rKs])
    KwT = p.sb("KwT", [128, T], BF16); rKw = Res("KwT")
    p.pool.op(lambda e: e.memset(KwT[64:128, :], 0.0), writes=[rKw])
    p.pool.dma(KwT[0:64, :], kwT[:, :], writes=[rKw])
    Vsa = p.sb("Vsa", [128, NB, 65], BF16); rVs = Res("Vsa")
    Vwa = p.sb("Vwa", [128, NB, 65], BF16); rVw = Res("Vwa")
    for (V_, rV_, vd) in ((Vsa, rVs, vs), (Vwa, rVw, vw)):
        p.pool.op(lambda e: e.memset(V_[:, :, 64:65], 1.0), writes=[rV_])
        p.pool.dma(V_[:, :, 0:64], vd.rearrange("(n p) c -> p n c", p=128), writes=[rV_])
    Qr = Ring([(p.sb(f"nQ{i}", [128, NH, 512], BF16), Res(f"nQ{i}")) for i in range(2)])
    for (t_, r_) in Qr.items:
        p.pool.op(lambda e: e.memset(t_[64:128, :, :], 0.0), writes=[r_])
    Bc = Ring([(p.sb(f"nBc{i}", [128, 2, 512], BF16), Res(f"nBc{i}")) for i in range(4)])
    Bw = Ring([(p.sb(f"nBw{i}", [128, 1408], BF16), Res(f"nBw{i}")) for i in range(3)])
    Bs = Ring([(p.sb(f"nBs{i}", [128, 2048], BF16), Res(f"nBs{i}")) for i in range(3)])
    gsr = Ring([(p.sb(f"ngs{i}", [NG, 512], F32), Res(f"ngs{i}")) for i in range(2)])
    ABr = Ring([(p.sb(f"nA{i}", [128, 256], F32), Res(f"nA{i}"), p.sb(f"nB{i}", [128, 256], F32), Res(f"nB{i}"))
                for i in range(2)])
    acc = p.sb("nacc", [64, NH, 512], F32)
    raccs = [Res(f"nacc{h}") for h in range(NH)]
    impr = Ring([(p.sb(f"nimp{i}", [64, 512], F32), Res(f"nimp{i}")) for i in range(2)])
    MbTr = Ring([(p.sb(f"nMbT{i}", [128, 512], BF16), Res(f"nMbT{i}")) for i in range(2)])
    for (t_, r_) in MbTr.items:
        p.pool.op(lambda e: e.memset(t_[64:128, :], 0.0), writes=[r_])
    pu = p.ps("npu", [64, 512], F32); rpu = Res("npu")
    gbc = p.ps("ngbc", [65, 512], F32); rgbc = Res("ngbc")
    S = dict(NG=NG, selg=selg, rselg=rselg, gbc=(gbc, rgbc),
             pusr=Ring([(p.sb(f"npus{i}", [64, 512], F32), Res()) for i in range(2)]),
             tmpr=Ring([(p.sb(f"ntmp{i}", [64, 512], F32), Res()) for i in range(3)]),
             rd2r=Ring([(p.sb(f"nrd2{i}", [65, 512], F32), Res()) for i in range(3)]))
    impm = p.sb("tk_impm", [128, 256], F32); rimpm = Res()
    work = p.sb("tk_work", [128, 256], F32); rwork = Res()
    t1 = p.sb("tk_t1", [128, 256], F32); rt1 = Res()
    Mb = p.sb("tk_Mb", [128, 256], F32); rMb = Res()
    m1 = p.sb("tk_m1", [128, 8], F32); rm1 = Res()
    m2 = p.sb("tk_m2", [128, 8], F32); rm2 = Res()
    thr = p.sb("tk_thr", [128, 1], F32); rthr = Res()

    groups = []
    for qc in range(NQC):
        Q, rQ = Qr.next(); gs, rgs = gsr.next(); A, rA, B, rB = ABr.next()
        imp, rimp = impr.next(); MbT, rMbT = MbTr.next()

        def load_qc(qc=qc, Q=Q, rQ=rQ, gs=gs, rgs=rgs, A=A, rA=rA, B=B, rB=rB):
            p.pool.dma(Q[0:64, :, :], nqv[:, :, qc * 512:(qc + 1) * 512], writes=[rQ])
            p.sp.dma(gs[:, :], gTd[:, qc * 512:(qc + 1) * 512], writes=[rgs])
            p.act.op(lambda e: e.activation(out=gs[:, :], in_=gs[:, :], func=AF.Sigmoid), reads=[rgs], writes=[rgs])
            p.sp.dma(A[:, :], tkA_d[:, 4 * qc:4 * qc + 4, :].rearrange("p a s -> p (a s)"), writes=[rA])
            p.sp.dma(B[:, :], tkB_d[:, 4 * qc:4 * qc + 4, :].rearrange("p a s -> p (a s)"), writes=[rB])

        def topk1(qc=qc, imp=imp, rimp=rimp, A=A, rA=rA, B=B, rB=rB):
            pt, rpt = ctx.sr.next()
            for qb in range(4):
                p.pe.op(lambda e: e.transpose(pt[:, qb * 64:(qb + 1) * 64], imp[:, qb * 128:(qb + 1) * 128],
                                              identf[0:64, 0:64]), reads=[rimp, ridf], writes=[rpt], inc=(qb == 3))
            p.dve.op(lambda e: e.tensor_tensor(out=impm[:], in0=pt[:, 0:256], in1=A[:, :], op=ALU.mult),
                     reads=[rpt, rA], writes=[rimpm])
            p.dve.op(lambda e: e.tensor_tensor(out=impm[:], in0=impm[:], in1=B[:, :], op=ALU.add),
                     reads=[rimpm, rB], writes=[rimpm])
            for qb in range(4):
                sl = slice(qb * 64, (qb + 1) * 64)
                p.dve.op(lambda e: e.max(out=m1[:, :], in_=impm[:, sl]), reads=[rimpm], writes=[rm1])
                p.dve.op(lambda e: e.match_replace(out=work[:, sl], in_to_replace=m1[:, :], in_values=impm[:, sl],
                                                   imm_value=-3.0e38), reads=[rimpm, rm1], writes=[rwork])
                p.dve.op(lambda e: e.max(out=m2[:, :], in_=work[:, sl]), reads=[rwork], writes=[rm2])
                p.dve.op(lambda e: e.tensor_reduce(out=thr[:, :], in_=m2[:, :], axis=AX.X, op=ALU.min),
                         reads=[rm2], writes=[rthr])
                p.dve.op(lambda e: e.tensor_scalar(out=t1[:, sl], in0=impm[:, sl], scalar1=thr[:, 0:1], scalar2=None,
                                                   op0=ALU.is_ge), reads=[rimpm, rthr], writes=[rt1])
            p.dve.op(lambda e: e.scalar_tensor_tensor(out=t1[:], in0=impm[:], scalar=-1.0e29, in1=t1[:],
                                                      op0=ALU.is_gt, op1=ALU.mult), reads=[rimpm, rt1], writes=[rt1])
            p.dve.op(lambda e: e.tensor_scalar(out=Mb[:], in0=t1[:], scalar1=1.0, scalar2=-NEG,
                                               op0=ALU.subtract, op1=ALU.mult), reads=[rt1], writes=[rMb])

        def topk2(MbT=MbT, rMbT=rMbT):
            pt2, rpt2 = ctx.sr.next()
            for qb in range(4):
                p.pe.op(lambda e: e.transpose(pt2[0:64, qb * 128:(qb + 1) * 128], Mb[:, qb * 64:(qb + 1) * 64],
                                              identf[:, :]), reads=[rMb, ridf], writes=[rpt2], inc=(qb == 3))
            p.act.op(lambda e: e.copy(out=MbT[0:64, :], in_=pt2[0:64, :]), reads=[rpt2], writes=[rMbT])

        for h in range(NH):
            bc_, rbc_ = Bc.next()
            ccs = [0] if qc < 4 else [0, 1]

            def loads(h=h, qc=qc, bc_=bc_, rbc_=rbc_, ccs=ccs, first=(h == 0), lq=load_qc):
                if first:
                    lq()
                for cc in ccs:
                    m = qc - 4 * cc
                    p.pool.dma(bc_[:, cc, :], Wc[h, :, 512 * m:512 * m + 512], writes=[rbc_])
            po, rpo = por.next()
            jobs = []
            for cc in ccs:
                mm = [(kcmpT[:, cc * 128:(cc + 1) * 128], Q[:, h, :], [rkc, rQ]),
                      (ident8[:], bc_[:, cc, :], [rid8, rbc_])]
                jb = dict(s_mms=mm, kp=128, scale=0.125,
                          pv=[(po[0:65, :], vcmpa[:, cc, 0:65], rpo, [rvc], cc == 0, cc == ccs[-1]),
                              (pu[0:64, :], ovl[:, cc, :], rpu, [rovl], cc == 0, cc == ccs[-1])])
                if cc == ccs[-1]:
                    jb["done"] = make_gated_done(p, nz, S, po, rpo, h, 0, gs, rgs, True, acc[:, h, :], raccs[h],
                                                 cmp_extra=(pu, rpu, imp, rimp))
                jobs.append(jb)
            groups.append(dict(loads=loads, jobs=jobs))
        for h in range(NH):
            bw_, rbw_ = Bw.next()

            def loads(h=h, bw_=bw_, rbw_=rbw_):
                p.pool.dma(bw_[:, :], Wwin[h, :, :], writes=[rbw_])
            po, rpo = por.next()
            kbs = list(range(max(0, 4 * qc - 4), 4 * qc + 4))
            jobs = []
            for kb in kbs:
                r = kb - 4 * qc
                mm = [(KwT[:, kb * 128:(kb + 1) * 128], Q[:, h, :], [rKw, rQ]),
                      (ident8[:], bw_[:, 128 * (3 - r):128 * (3 - r) + 512], [rid8, rbw_])]
                jb = dict(s_mms=mm, kp=128, scale=0.125,
                          pv=[(po[0:65, :], Vwa[:, kb, 0:65], rpo, [rVw], kb == kbs[0], kb == kbs[-1])])
                if kb == kbs[-1]:
                    jb["done"] = make_gated_done(p, nz, S, po, rpo, h, 2, gs, rgs, False, acc[:, h, :], raccs[h])
                jobs.append(jb)
            if h == 1:
                jobs[0]["pre2"] = topk1
            if h == 4:
                jobs[0]["pre2"] = topk2
            groups.append(dict(loads=loads, jobs=jobs))
        for h in range(NH):
            bs_, rbs_ = Bs.next()

            def loads(h=h, bs_=bs_, rbs_=rbs_):
                p.pool.dma(bs_[:, :], Wsel[h, :, :], writes=[rbs_])
            po, rpo = por.next()
            kbs = list(range(0, 4 * qc + 4))
            jobs = []
            for kb in kbs:
                r = max(kb - 4 * qc, -9)
                mm = [(KsT[:, kb * 128:(kb + 1) * 128], Q[:, h, :], [rKs, rQ]),
                      (esel[:, kb * 128:(kb + 1) * 128], MbT[:, :], [resel, rMbT]),
                      (ident8[:], bs_[:, 128 * (3 - r):128 * (3 - r) + 512], [rid8, rbs_])]
                jb = dict(s_mms=mm, kp=128, scale=0.125,
                          pv=[(po[0:65, :], Vsa[:, kb, 0:65], rpo, [rVs], kb == kbs[0], kb == kbs[-1])])
                if kb == kbs[-1]:
                    fin = (lambda h=h, qc=qc: out_tags.append(
                        p.sp.dma(OTn[:, h, qc * 512:(qc + 1) * 512], acc[:, h, :], reads=[raccs[h]])))
                    jb["done"] = make_gated_done(p, nz, S, po, rpo, h, 1, gs, rgs, False, acc[:, h, :], raccs[h],
                                                 final=fin)
                jobs.append(jb)
            groups.append(dict(loads=loads, jobs=jobs))
    PD = 2
    flat = []
    for gi, g in enumerate(groups):
        if gi + PD < len(groups):
            prev = g["jobs"][0].get("pre")
            nxt = groups[gi + PD]["loads"]
            g["jobs"][0]["pre"] = (lambda a=prev, b=nxt: ((a() if a else None), b()))
        flat += g["jobs"]
    for gi in range(min(PD, len(groups))):
        groups[gi]["loads"]()
    run_jobs(ctx, flat)


_PROGS = {}


def _prog(key, fn):
    if key not in _PROGS:
        _PROGS[key] = fn()
    return _PROGS[key]


EVEN_FM = [(0, 1280), (1536, 2944), (3072, 3200), (3328, 3376)]
EVEN_TM = [(1280, 1536), (2944, 3072), (3200, 3328)]
ODD_FM = [(0, 4096), (6144, 6176)]
ODD_TM = [(4096, 6144)]


def _gT(g):
    return np.ascontiguousarray(np.asarray(g, np.float32).reshape(16, 128).T)


def _launch(nc, in_maps):
    res = run_bass_kernel_spmd(nc, in_maps, core_ids=list(range(len(in_maps))))
    return res.results


def kernel_unfused(x, rel_bias, norm_mix, norm_ffn, norm_final, w_in_even, w_out_even, a_sinks, nsa_pe_k, nsa_pe_v,
           nsa_cmp_k_w1, nsa_cmp_k_w2, nsa_cmp_v_w1, nsa_cmp_v_w2, w_in_odd, w_out_odd, fox_fgate_b,
           w_ffn_up, w_ffn_down):
    f32 = lambda a: np.ascontiguousarray(np.asarray(a, dtype=np.float32))
    x = f32(x)
    rel_bias = f32(rel_bias)
    B, T, D = x.shape
    TH = T // 2
    DEPTH = norm_mix.shape[0]
    xs = [np.ascontiguousarray(x[c // 2, (c % 2) * TH:(c % 2 + 1) * TH, :]) for c in range(8)]
    consts = nsa_consts()
    cmask = causal_masks()
    Wswa = [skew_bias(rel_bias[:, h], 3, 1024, 0, 128) for h in range(16)]
    Wc = [cmp_bias(rel_bias[:, 16 + h]) for h in range(16)]
    Wsel = [skew_bias(rel_bias[:, 16 + h], 3, 2048, 0, 1 << 30) for h in range(16)]
    Wwin = [skew_bias(rel_bias[:, 16 + h], 3, 1408, 0, 512) for h in range(16)]
    fm = lambda a: np.ascontiguousarray(a.T)
    for l in range(DEPTH):
        even = (l % 2 == 0)
        i = l // 2
        w_in = f32(w_in_even[i]) if even else f32(w_in_odd[i])
        n_in = w_in.shape[1]
        nc = _prog(("proj", even), lambda: build_proj(n_in, EVEN_FM if even else ODD_FM, EVEN_TM if even else ODD_TM,
                                                      T=TH, name="proj_e" if even else "proj_o"))
        gT = _gT(norm_mix[l])
        res = _launch(nc, [{"x": xs[c], "gT": gT, "w": w_in} for c in range(8)])
        YT = [np.concatenate([res[2 * b]["outT"], res[2 * b + 1]["outT"]], axis=1) for b in range(B)]
        V = [np.concatenate([res[2 * b]["outV"], res[2 * b + 1]["outV"]], axis=0) for b in range(B)]
        del res
        ca = np.ascontiguousarray
        if even:
            nc = _prog("even", lambda: build_even(T=T))
            maps = []
            for c in range(8):
                b, hf = c // 2, c % 2
                Y, Vb = YT[b], V[b]
                m = {"sqT": ca(Y[512 * hf:512 * hf + 512]), "skT": ca(Y[1024 + 128 * hf:1024 + 128 * hf + 128]),
                     "sv": ca(Vb[:, 128 * hf:128 * hf + 128]), "sinks": f32(a_sinks[i][8 * hf:8 * hf + 8]).reshape(1, 8),
                     "Wswa": np.stack(Wswa[8 * hf:8 * hf + 8]),
                     "nqT": ca(Y[1280 + 512 * hf:1280 + 512 * hf + 512]),
                     "kcT": ca(Y[2304 + 64 * hf:2304 + 64 * hf + 64]), "vcT": ca(Y[2432 + 64 * hf:2432 + 64 * hf + 64]),
                     "ksT": ca(Y[2560 + 64 * hf:2560 + 64 * hf + 64]), "vs": ca(Vb[:, 256 + 64 * hf:256 + 64 * hf + 64]),
                     "kwT": ca(Y[2688 + 64 * hf:2688 + 64 * hf + 64]), "vw": ca(Vb[:, 384 + 64 * hf:384 + 64 * hf + 64]),
                     "gT": ca(Y[2816 + 24 * hf:2816 + 24 * hf + 24]),
                     "pekT": fm(f32(nsa_pe_k[i])), "pevT": fm(f32(nsa_pe_v[i])),
                     "w1k": f32(nsa_cmp_k_w1[i]), "w2k": f32(nsa_cmp_k_w2[i]),
                     "w1v": f32(nsa_cmp_v_w1[i]), "w2v": f32(nsa_cmp_v_w2[i]),
                     "Wc": np.stack(Wc[8 * hf:8 * hf + 8]), "Wsel": np.stack(Wsel[8 * hf:8 * hf + 8]),
                     "Wwin": np.stack(Wwin[8 * hf:8 * hf + 8])}
                m.update(consts)
                maps.append(m)
            res = _launch(nc, maps)
            OTb = []
            for b in range(B):
                o = np.empty((2048, T), np.float32)
                for hf in range(2):
                    r = res[2 * b + hf]["OT"]
                    o[512 * hf:512 * hf + 512] = r[0:512]
                    o[1024 + 512 * hf:1024 + 512 * hf + 512] = r[512:1024]
                OTb.append(o)
            w_out = f32(w_out_even[i])
        else:
            nc = _prog("fox", lambda: build_fox(16, T=T))
            maps = []
            for c in range(8):
                b, hf = c // 2, c % 2
                Y, Vb = YT[b], V[b]
                maps.append({"qT": ca(Y[1024 * hf:1024 * hf + 1024]), "kT": ca(Y[2048 + 1024 * hf:2048 + 1024 * hf + 1024]),
                             "v": ca(Vb[:, 1024 * hf:1024 * hf + 1024]), "fT": ca(Y[4096 + 16 * hf:4096 + 16 * hf + 16]),
                             "fb": f32(fox_fgate_b[i][16 * hf:16 * hf + 16]).reshape(16, 1), "masks": cmask})
            res = _launch(nc, maps)
            OTb = [np.concatenate([res[2 * b]["OT"], res[2 * b + 1]["OT"]], axis=0) for b in range(B)]
            w_out = f32(w_out_odd[i])
        del res, YT, V
        final = (l == DEPTH - 1)
        nc = _prog(("post", final), lambda: build_post(final, T=TH, name="post_f" if final else "post"))
        gT = _gT(norm_ffn[l])
        wup = f32(w_ffn_up[l]); wdn = f32(w_ffn_down[l])
        maps = []
        for c in range(8):
            b, th = c // 2, c % 2
            m = {"x": xs[c], "OT": ca(OTb[b][:, th * TH:(th + 1) * TH]), "wo": w_out, "gT": gT, "wup": wup, "wdn": wdn}
            if final:
                m["gf"] = ca(np.broadcast_to(f32(norm_final), (128, D)))
            maps.append(m)
        res = _launch(nc, maps)
        xs = [res[c]["xo"] for c in range(8)]
        del res, OTb
    out = np.empty((B, T, D), np.float32)
    for c in range(8):
        out[c // 2, (c % 2) * TH:(c % 2 + 1) * TH, :] = xs[c]
    return out


def build_mega(T=4096, DEPTH=4, name="mega"):
    import contextlib
    p = Prog(name)
    nc = p.nc
    D = D_MODEL

    def din(nm, shape):
        return nc.dram_tensor(nm, list(shape), F32, kind="ExternalInput").ap()
    x_in = din("x", [T, D])
    gfin = din("gfin", [128, D])
    L = []
    for l in range(DEPTH):
        even = (l % 2 == 0)
        d = dict(gm=din(f"gm{l}", [128, 16]), gf=din(f"gf{l}", [128, 16]),
                 win=din(f"win{l}", [D, 3376 if even else 6176]), wout=din(f"wout{l}", [D, D]),
                 wup=din(f"wup{l}", [D, 4 * D]), wdn=din(f"wdn{l}", [4 * D, D]))
        if even:
            d.update(sinks=din(f"sinks{l}", [1, 16]), pekT=din(f"pekT{l}", [64, 32]), pevT=din(f"pevT{l}", [64, 32]),
                     w1k=din(f"w1k{l}", [2048, 256]), w2k=din(f"w2k{l}", [256, 64]),
                     w1v=din(f"w1v{l}", [2048, 256]), w2v=din(f"w2v{l}", [256, 64]))
        else:
            d.update(fb=din(f"fb{l}", [32, 1]))
        L.append(d)
    Wswa = din("Wswa", [16, 128, 1024]); Wc = din("Wc", [16, 128, 4096])
    Wsel = din("Wsel", [16, 128, 2048]); Wwin = din("Wwin", [16, 128, 1408])
    cst = dict(ovl=din("ovl", [128, 2, 64]), esel=din("esel", [64, 4096]), tkA=din("tkA", [128, 32, 64]),
               tkB=din("tkB", [128, 32, 64]), selg=din("selg", [24, 24 * 65]))
    masks = din("masks", [128, 4, 512])
    out = nc.dram_tensor("out", [T, D], F32, kind="ExternalOutput").ap()
    xbuf = nc.dram_tensor("xbuf", [T, D], F32).ap()
    YT = nc.dram_tensor("YTbuf", [4128, T], F32).ap()
    Vb = nc.dram_tensor("Vbuf", [T, 2048], F32).ap()
    OTb = nc.dram_tensor("OTbuf", [2048, T], F32).ap()
    shared = {"out_tags": []}
    TH = T // 2
    with contextlib.ExitStack() as st:
        p.start(st)
        for l in range(DEPTH):
            even = (l % 2 == 0)
            d = L[l]
            xsrc = x_in if l == 0 else xbuf
            fm, tm = (EVEN_FM, EVEN_TM) if even else (ODD_FM, ODD_TM)
            n_fm = sum(b - a for a, b in fm); n_tm = sum(b - a for a, b in tm)
            for hh in range(2):
                p.pfx = f"L{l}p{hh}_"
                build_proj(3376 if even else 6176, fm, tm, T=TH,
                           host=(p, dict(x=xsrc[hh * TH:(hh + 1) * TH, :], gT=d["gm"], w=d["win"],
                                         outT=YT[0:n_fm, hh * TH:(hh + 1) * TH],
                                         outV=Vb[hh * TH:(hh + 1) * TH, 0:n_tm])))
            if even:
                for hf in range(2):
                    p.pfx = f"L{l}e{hf}_"
                    a = dict(sqT=YT[512 * hf:512 * hf + 512], skT=YT[1024 + 128 * hf:1024 + 128 * hf + 128],
                             sv=Vb[:, 128 * hf:128 * hf + 128], sinks=d["sinks"][:, 8 * hf:8 * hf + 8],
                             Wswa=Wswa[8 * hf:8 * hf + 8], nqT=YT[1280 + 512 * hf:1280 + 512 * hf + 512],
                             kcT=YT[2304 + 64 * hf:2304 + 64 * hf + 64], vcT=YT[2432 + 64 * hf:2432 + 64 * hf + 64],
                             ksT=YT[2560 + 64 * hf:2560 + 64 * hf + 64], vs=Vb[:, 256 + 64 * hf:256 + 64 * hf + 64],
                             kwT=YT[2688 + 64 * hf:2688 + 64 * hf + 64], vw=Vb[:, 384 + 64 * hf:384 + 64 * hf + 64],
                             gT=YT[2816 + 24 * hf:2816 + 24 * hf + 24], pekT=d["pekT"], pevT=d["pevT"],
                             w1k=d["w1k"], w2k=d["w2k"], w1v=d["w1v"], w2v=d["w2v"],
                             Wc=Wc[8 * hf:8 * hf + 8], Wsel=Wsel[8 * hf:8 * hf + 8], Wwin=Wwin[8 * hf:8 * hf + 8],
                             OT_swa=OTb[512 * hf:512 * hf + 512], OT_nsa=OTb[1024 + 512 * hf:1024 + 512 * hf + 512])
                    a.update(cst)
                    build_even(T=T, host=(p, a))
            else:
                p.pfx = f"L{l}f_"
                build_fox(32, T=T, host=(p, dict(qT=YT[0:2048], kT=YT[2048:4096], v=Vb[:, 0:2048], fT=YT[4096:4128],
                                                 fb=d["fb"], masks=masks, OT=OTb[0:2048])))
            final = (l == DEPTH - 1)
            p.pfx = f"L{l}o_"
            a = dict(x=xsrc, OT=OTb, wo=d["wout"], gT=d["gf"], wup=d["wup"], wdn=d["wdn"],
                     xo=(out if final else xbuf), out_tags=shared["out_tags"])
            if final:
                a["gf"] = gfin
            build_post(final, T=T, host=(p, a))
        p.finish(shared["out_tags"])
    return nc


def mega_inputs(inp, DEPTH=4):
    f32 = lambda a: np.ascontiguousarray(np.asarray(a, dtype=np.float32))
    fm = lambda a: np.ascontiguousarray(np.asarray(a, dtype=np.float32).T)
    rel_bias = f32(inp["rel_bias"])
    m = {}
    for l in range(DEPTH):
        i = l // 2
        even = (l % 2 == 0)
        m[f"gm{l}"] = _gT(inp["norm_mix"][l]); m[f"gf{l}"] = _gT(inp["norm_ffn"][l])
        m[f"win{l}"] = f32(inp["w_in_even"][i]) if even else f32(inp["w_in_odd"][i])
        m[f"wout{l}"] = f32(inp["w_out_even"][i]) if even else f32(inp["w_out_odd"][i])
        m[f"wup{l}"] = f32(inp["w_ffn_up"][l]); m[f"wdn{l}"] = f32(inp["w_ffn_down"][l])
        if even:
            m[f"sinks{l}"] = f32(inp["a_sinks"][i]).reshape(1, 16)
            m[f"pekT{l}"] = fm(inp["nsa_pe_k"][i]); m[f"pevT{l}"] = fm(inp["nsa_pe_v"][i])
            m[f"w1k{l}"] = f32(inp["nsa_cmp_k_w1"][i]); m[f"w2k{l}"] = f32(inp["nsa_cmp_k_w2"][i])
            m[f"w1v{l}"] = f32(inp["nsa_cmp_v_w1"][i]); m[f"w2v{l}"] = f32(inp["nsa_cmp_v_w2"][i])
        else:
            m[f"fb{l}"] = f32(inp["fox_fgate_b"][i]).reshape(32, 1)
    m["gfin"] = np.ascontiguousarray(np.broadcast_to(f32(inp["norm_final"]), (128, D_MODEL)))
    m["Wswa"] = np.stack([skew_bias(rel_bias[:, h], 3, 1024, 0, 128) for h in range(16)])
    m["Wc"] = np.stack([cmp_bias(rel_bias[:, 16 + h]) for h in range(16)])
    m["Wsel"] = np.stack([skew_bias(rel_bias[:, 16 + h], 3, 2048, 0, 1 << 30) for h in range(16)])
    m["Wwin"] = np.stack([skew_bias(rel_bias[:, 16 + h], 3, 1408, 0, 512) for h in range(16)])
    m.update(nsa_consts())
    m["masks"] = causal_masks()
    return m


def kernel(**inputs):
    x = np.ascontiguousarray(np.asarray(inputs["x"], dtype=np.float32))
    B, T, D = x.shape
    DEPTH = int(np.asarray(inputs["norm_mix"]).shape[0])
    nc = _prog("mega", lambda: build_mega(T=T, DEPTH=DEPTH))
    shared = mega_inputs(inputs, DEPTH)
    in_maps = []
    for b in range(B):
        m = dict(shared)
        m["x"] = np.ascontiguousarray(x[b])
        in_maps.append(m)
    res = run_bass_kernel_spmd(nc, in_maps, core_ids=list(range(B)))
    return np.stack([res.results[b]["out"] for b in range(B)], axis=0).astype(np.float32)
```
